# Optimizing a Trainium2 kernel written in Bass

```python
import math
import jax, jax.numpy as jnp
from jax import lax
import numpy as np

D_MODEL = 1024
BATCH = 2
SEQ = 8192
DEPTH = 1

HEAD_DIM = 64
MOBA_HEADS = 8
MOBA_BLOCK = 256
MOBA_TOPK = 3
NSA_HEADS = 8
NSA_KV_GROUPS = 2
NSA_HPG = NSA_HEADS // NSA_KV_GROUPS
NSA_CMP_BLOCK = 32
NSA_CMP_STRIDE = 16
NSA_CMP_HIDDEN = 256
NSA_SLC_BLOCK = 64
NSA_SLC_TOPN = 16
NSA_WINDOW = 512
REL_BUCKETS = 32
REL_MAX_DIST = 128
MEM_LEN = 256
XATTN_HEADS = 4
XATTN_HEAD_DIM = 128
D_FF = 2816
CONV_WIDTH = 3
Q_BLOCK = 128
RMS_EPS = 1e-6
NEG_BIG = -1e30

MOBA_W = MOBA_HEADS * HEAD_DIM
NSA_W = NSA_HEADS * HEAD_DIM
NSA_KV_W = NSA_KV_GROUPS * HEAD_DIM
XATTN_W = XATTN_HEADS * XATTN_HEAD_DIM
IN_SPLITS = (MOBA_W, MOBA_W, MOBA_W, NSA_W, NSA_KV_W, NSA_KV_W, NSA_KV_W, NSA_KV_W, NSA_KV_W, NSA_KV_W, 3 * NSA_HEADS, D_MODEL, D_MODEL)
IN_WIDTH = sum(IN_SPLITS)

kernel_name = 'hybrid_moba_nsa_gated_block'


def rms_norm(x, g):
    xf = x.astype(jnp.float32)
    y = xf * lax.rsqrt(jnp.mean(xf * xf, axis=-1, keepdims=True) + RMS_EPS)
    return (y * g.astype(jnp.float32)).astype(x.dtype)


def t5_bucket(dist):
    n = jnp.maximum(dist, 0)
    exact = REL_BUCKETS // 2
    nf = jnp.maximum(n, 1).astype(jnp.float32)
    large = exact + (jnp.log(nf / exact) / math.log(REL_MAX_DIST / exact) * (REL_BUCKETS - exact)).astype(jnp.int32)
    large = jnp.minimum(large, REL_BUCKETS - 1)
    return jnp.where(n < exact, n, large)


def masked_softmax(s, mask):
    s = jnp.where(mask, s.astype(jnp.float32), NEG_BIG)
    p = jax.nn.softmax(s, axis=-1)
    return jnp.where(jnp.any(mask, axis=-1, keepdims=True), p, 0.0)


def to_heads(t, n, dh):
    b, s, _ = t.shape
    return t.reshape(b, s, n, dh).transpose(0, 2, 1, 3)


def from_heads(t):
    b, n, s, dh = t.shape
    return t.transpose(0, 2, 1, 3).reshape(b, s, n * dh)


def moba_attention(q, k, v, rel_bias):
    b, h, s, dh = q.shape
    nb = -(-s // MOBA_BLOCK)
    sp = nb * MOBA_BLOCK
    pad = ((0, 0), (0, 0), (0, sp - s), (0, 0))
    k = jnp.pad(k, pad)
    v = jnp.pad(v, pad)
    scale = dh ** -0.5
    topk = min(MOBA_TOPK, nb)
    k_blk = k.reshape(b, h, nb, MOBA_BLOCK, dh)
    v_blk = v.reshape(b, h, nb, MOBA_BLOCK, dh)
    k_mean = jnp.mean(k_blk.astype(jnp.float32), axis=3).astype(k.dtype)
    bi = jnp.arange(b)[:, None, None, None]
    hi = jnp.arange(h)[None, :, None, None]
    blk_off = jnp.arange(MOBA_BLOCK)
    kb = topk * MOBA_BLOCK

    def query_block(ci):
        start = ci * Q_BLOCK
        qc = lax.dynamic_slice_in_dim(q, start, Q_BLOCK, axis=2)
        t = start + jnp.arange(Q_BLOCK)
        cur = start // MOBA_BLOCK
        gate = jnp.einsum('bhqd,bhnd->bhqn', qc, k_mean).astype(jnp.float32)
        gate = jnp.where(jnp.arange(nb) < cur, gate, -jnp.inf)
        _, sel = lax.top_k(gate, topk)
        k_sel = k_blk[bi, hi, sel]
        v_sel = v_blk[bi, hi, sel]
        pos_sel = sel[..., None] * MOBA_BLOCK + blk_off
        s_sel = jnp.einsum('bhqd,bhqjkd->bhqjk', qc, k_sel).astype(jnp.float32) * scale
        s_sel = s_sel + rel_bias[t5_bucket(t[:, None, None] - pos_sel), hi[..., None]]
        m_sel = jnp.broadcast_to((sel < cur)[..., None], pos_sel.shape)
        k_own = lax.dynamic_slice_in_dim(k, cur * MOBA_BLOCK, MOBA_BLOCK, axis=2)
        v_own = lax.dynamic_slice_in_dim(v, cur * MOBA_BLOCK, MOBA_BLOCK, axis=2)
        pos_own = cur * MOBA_BLOCK + blk_off
        s_own = jnp.einsum('bhqd,bhkd->bhqk', qc, k_own).astype(jnp.float32) * scale
        s_own = s_own + rel_bias[t5_bucket(t[:, None] - pos_own[None, :])].transpose(2, 0, 1)[None]
        m_own = jnp.broadcast_to(pos_own[None, :] <= t[:, None], s_own.shape)
        scores = jnp.concatenate([s_sel.reshape(b, h, Q_BLOCK, kb), s_own], axis=-1)
        mask = jnp.concatenate([m_sel.reshape(b, h, Q_BLOCK, kb), m_own], axis=-1)
        p = masked_softmax(scores, mask).astype(v.dtype)
        o = jnp.einsum('bhqjk,bhqjkd->bhqd', p[..., :kb].reshape(b, h, Q_BLOCK, topk, MOBA_BLOCK), v_sel)
        return o + jnp.einsum('bhqk,bhkd->bhqd', p[..., kb:], v_own)

    out = lax.map(query_block, jnp.arange(s // Q_BLOCK))
    return out.transpose(1, 2, 0, 3, 4).reshape(b, h, s, dh)


def compress_blocks(kv, pos_emb, w1, w2):
    b, g, s, dh = kv.shape
    nc = (s - NSA_CMP_BLOCK) // NSA_CMP_STRIDE + 1
    idx = jnp.arange(nc)[:, None] * NSA_CMP_STRIDE + jnp.arange(NSA_CMP_BLOCK)[None, :]
    blocks = (kv[:, :, idx] + pos_emb).reshape(b, g, nc, NSA_CMP_BLOCK * dh)
    return jax.nn.gelu(blocks @ w1) @ w2


def nsa_compressed_selected(qg, k_c, v_c, k_s, v_s, rb, pos_k, w1_k, w2_k, pos_v, w1_v, w2_v):
    b, g, r, s, dh = qg.shape
    scale = dh ** -0.5
    kc = compress_blocks(k_c, pos_k, w1_k, w2_k)
    vc = compress_blocks(v_c, pos_v, w1_v, w2_v)
    nc = kc.shape[2]
    c_end = jnp.arange(nc) * NSA_CMP_STRIDE + NSA_CMP_BLOCK - 1
    nsb = s // NSA_SLC_BLOCK
    topn = min(NSA_SLC_TOPN, nsb)
    ratio = NSA_SLC_BLOCK // NSA_CMP_STRIDE
    cover_off = np.array([m - n for m in range(ratio) for n in range(NSA_CMP_BLOCK // NSA_CMP_STRIDE)], dtype=np.int32)
    cidx = jnp.arange(nsb)[:, None] * ratio + jnp.asarray(cover_off)[None, :]
    c_ok = (cidx >= 0) & (cidx < nc)
    cidx = jnp.clip(cidx, 0, nc - 1)
    ks_blk = k_s.reshape(b, g, nsb, NSA_SLC_BLOCK, dh)
    vs_blk = v_s.reshape(b, g, nsb, NSA_SLC_BLOCK, dh)
    bi = jnp.arange(b)[:, None, None, None]
    gi = jnp.arange(g)[None, :, None, None]
    gi6 = jnp.arange(g)[None, :, None, None, None, None]
    ri6 = jnp.arange(r)[None, None, :, None, None, None]
    slc_off = jnp.arange(NSA_SLC_BLOCK)
    jblk = jnp.arange(nsb)
    nk = topn * NSA_SLC_BLOCK

    def query_block(ci):
        start = ci * Q_BLOCK
        qc = lax.dynamic_slice_in_dim(qg, start, Q_BLOCK, axis=3)
        t = start + jnp.arange(Q_BLOCK)
        s_c = jnp.einsum('bgrqd,bgnd->bgrqn', qc, kc).astype(jnp.float32) * scale
        p_c = masked_softmax(s_c, c_end[None, :] <= t[:, None])
        o_c = jnp.einsum('bgrqn,bgnd->bgrqd', p_c.astype(vc.dtype), vc)
        p_grp = jnp.sum(p_c, axis=2)
        imp = jnp.sum(jnp.where(c_ok, p_grp[..., cidx], 0.0), axis=-1)
        cur = t // NSA_SLC_BLOCK
        forced = (jblk[None, :] == 0) | (jblk[None, :] == cur[:, None]) | (jblk[None, :] == cur[:, None] - 1)
        imp = jnp.where(forced, jnp.inf, jnp.where(jblk[None, :] <= cur[:, None], imp, -jnp.inf))
        _, sel = lax.top_k(imp, topn)
        k_sel = ks_blk[bi, gi, sel]
        v_sel = vs_blk[bi, gi, sel]
        pos = sel[..., None] * NSA_SLC_BLOCK + slc_off
        dist = t[:, None, None] - pos
        s_s = jnp.einsum('bgrqd,bgqjkd->bgrqjk', qc, k_sel).astype(jnp.float32) * scale
        s_s = s_s + rb[t5_bucket(dist)[:, :, None], gi6, ri6]
        m_s = ((sel <= cur[:, None])[..., None] & (dist >= 0))[:, :, None]
        m_s = jnp.broadcast_to(m_s, s_s.shape)
        p_s = masked_softmax(s_s.reshape(b, g, r, Q_BLOCK, nk), m_s.reshape(b, g, r, Q_BLOCK, nk)).astype(v_s.dtype)
        o_s = jnp.einsum('bgrqjk,bgqjkd->bgrqd', p_s.reshape(b, g, r, Q_BLOCK, topn, NSA_SLC_BLOCK), v_sel)
        return o_c, o_s

    o_c, o_s = lax.map(query_block, jnp.arange(s // Q_BLOCK))
    o_c = o_c.transpose(1, 2, 3, 0, 4, 5).reshape(b, g, r, s, dh)
    o_s = o_s.transpose(1, 2, 3, 0, 4, 5).reshape(b, g, r, s, dh)
    return o_c, o_s


def sliding_window_attention(qg, k, v, rb):
    b, g, r, s, dh = qg.shape
    nqb = s // Q_BLOCK
    span = NSA_WINDOW + Q_BLOCK
    pad = ((0, 0), (0, 0), (NSA_WINDOW, 0), (0, 0))
    idx = jnp.arange(nqb)[:, None] * Q_BLOCK + jnp.arange(span)[None, :]
    kb = jnp.pad(k, pad)[:, :, idx]
    vb = jnp.pad(v, pad)[:, :, idx]
    qb = qg.reshape(b, g, r, nqb, Q_BLOCK, dh)
    t = jnp.arange(nqb)[:, None] * Q_BLOCK + jnp.arange(Q_BLOCK)[None, :]
    pos = idx - NSA_WINDOW
    dist = t[:, :, None] - pos[:, None, :]
    mask = (pos[:, None, :] >= 0) & (dist >= 0) & (dist < NSA_WINDOW)
    bias = rb[t5_bucket(dist)].transpose(3, 4, 0, 1, 2)
    sc = jnp.einsum('bgrnqd,bgnkd->bgrnqk', qb, kb).astype(jnp.float32) * (dh ** -0.5) + bias
    p = masked_softmax(sc, mask).astype(v.dtype)
    return jnp.einsum('bgrnqk,bgnkd->bgrnqd', p, vb).reshape(b, g, r, s, dh)


def causal_depthwise_conv(u, w, bias):
    f = u.shape[-1]
    y = lax.conv_general_dilated(u, w[:, None, :].astype(u.dtype), window_strides=(1,),
                                 padding=[(CONV_WIDTH - 1, 0)], dimension_numbers=('NWC', 'WIO', 'NWC'),
                                 feature_group_count=f)
    return y + bias


def hybrid_layer(x, mem, rel_bias, norm_mix_g, w_in, cmp_pos_k, cmp_w1_k, cmp_w2_k, cmp_pos_v, cmp_w1_v, cmp_w2_v,
                 w_branch_a, w_branch_b, w_out, norm_xattn_g, norm_mem_g, w_xq, w_xkv, w_xo,
                 norm_ffn_g, w_gate, w_up, conv_w, conv_b, w_down):
    b, s, _ = x.shape
    g, r, dh = NSA_KV_GROUPS, NSA_HPG, HEAD_DIM
    h = rms_norm(x, norm_mix_g)
    cuts = [int(c) for c in np.cumsum(IN_SPLITS)[:-1]]
    (mq, mk, mv, nq, nkc, nvc, nks, nvs, nkw, nvw, ngate, gate_a, gate_b) = jnp.split(h @ w_in, cuts, axis=-1)
    rb_moba = rel_bias[:, :MOBA_HEADS]
    rb_nsa = rel_bias[:, MOBA_HEADS:].reshape(REL_BUCKETS, g, r)
    o_a = moba_attention(to_heads(mq, MOBA_HEADS, dh), to_heads(mk, MOBA_HEADS, dh), to_heads(mv, MOBA_HEADS, dh), rb_moba)
    qg = to_heads(nq, NSA_HEADS, dh).reshape(b, g, r, s, dh)
    o_cmp, o_slc = nsa_compressed_selected(qg, to_heads(nkc, g, dh), to_heads(nvc, g, dh), to_heads(nks, g, dh),
                                           to_heads(nvs, g, dh), rb_nsa, cmp_pos_k, cmp_w1_k, cmp_w2_k,
                                           cmp_pos_v, cmp_w1_v, cmp_w2_v)
    o_win = sliding_window_attention(qg, to_heads(nkw, g, dh), to_heads(nvw, g, dh), rb_nsa)
    gb = jax.nn.sigmoid(ngate).reshape(b, s, 3, g, r).transpose(2, 0, 3, 4, 1)[..., None]
    o_b = (gb[0] * o_cmp + gb[1] * o_slc + gb[2] * o_win).reshape(b, NSA_HEADS, s, dh)
    merged = jax.nn.sigmoid(gate_a) * (from_heads(o_a) @ w_branch_a) + jax.nn.sigmoid(gate_b) * (from_heads(o_b) @ w_branch_b)
    x = x + merged @ w_out
    hq = rms_norm(x, norm_xattn_g)
    hm = rms_norm(mem, norm_mem_g)
    xq = to_heads(hq @ w_xq, XATTN_HEADS, XATTN_HEAD_DIM)
    xk, xv = jnp.split(hm @ w_xkv, 2, axis=-1)
    xk = to_heads(xk, XATTN_HEADS, XATTN_HEAD_DIM)
    xv = to_heads(xv, XATTN_HEADS, XATTN_HEAD_DIM)
    sc = jnp.einsum('bhqd,bhkd->bhqk', xq, xk).astype(jnp.float32) * (XATTN_HEAD_DIM ** -0.5)
    p = jax.nn.softmax(sc, axis=-1).astype(xv.dtype)
    x = x + from_heads(jnp.einsum('bhqk,bhkd->bhqd', p, xv)) @ w_xo
    hf = rms_norm(x, norm_ffn_g)
    a = causal_depthwise_conv(hf @ w_gate, conv_w, conv_b)
    return x + (jax.nn.gelu(a) * (hf @ w_up)) @ w_down


def setup_inputs(seed: int = 0) -> dict:
    key = jax.random.key(seed)
    ks = jax.random.split(key, 26)
    L = DEPTH
    f32 = jnp.float32

    def normal(k, shape, scale):
        return jax.random.normal(k, shape, f32) * scale

    def gain(k, shape):
        return 1.0 + 0.01 * jax.random.normal(k, shape, f32)

    cmp_in = NSA_CMP_BLOCK * HEAD_DIM
    return {
        'x': normal(ks[0], (BATCH, SEQ, D_MODEL), 1.0),
        'mem': normal(ks[1], (BATCH, MEM_LEN, D_MODEL), 1.0),
        'rel_bias': normal(ks[2], (REL_BUCKETS, MOBA_HEADS + NSA_HEADS), 0.1),
        'norm_mix_g': gain(ks[3], (L, D_MODEL)),
        'w_in': normal(ks[4], (L, D_MODEL, IN_WIDTH), D_MODEL ** -0.5),
        'cmp_pos_k': normal(ks[5], (L, NSA_CMP_BLOCK, HEAD_DIM), 0.1),
        'cmp_w1_k': normal(ks[6], (L, cmp_in, NSA_CMP_HIDDEN), cmp_in ** -0.5),
        'cmp_w2_k': normal(ks[7], (L, NSA_CMP_HIDDEN, HEAD_DIM), NSA_CMP_HIDDEN ** -0.5),
        'cmp_pos_v': normal(ks[8], (L, NSA_CMP_BLOCK, HEAD_DIM), 0.1),
        'cmp_w1_v': normal(ks[9], (L, cmp_in, NSA_CMP_HIDDEN), cmp_in ** -0.5),
        'cmp_w2_v': normal(ks[10], (L, NSA_CMP_HIDDEN, HEAD_DIM), NSA_CMP_HIDDEN ** -0.5),
        'w_branch_a': normal(ks[11], (L, MOBA_W, D_MODEL), MOBA_W ** -0.5),
        'w_branch_b': normal(ks[12], (L, NSA_W, D_MODEL), NSA_W ** -0.5),
        'w_out': normal(ks[13], (L, D_MODEL, D_MODEL), D_MODEL ** -0.5),
        'norm_xattn_g': gain(ks[14], (L, D_MODEL)),
        'norm_mem_g': gain(ks[15], (L, D_MODEL)),
        'w_xq': normal(ks[16], (L, D_MODEL, XATTN_W), D_MODEL ** -0.5),
        'w_xkv': normal(ks[17], (L, D_MODEL, 2 * XATTN_W), D_MODEL ** -0.5),
        'w_xo': normal(ks[18], (L, XATTN_W, D_MODEL), XATTN_W ** -0.5),
        'norm_ffn_g': gain(ks[19], (L, D_MODEL)),
        'w_gate': normal(ks[20], (L, D_MODEL, D_FF), D_MODEL ** -0.5),
        'w_up': normal(ks[21], (L, D_MODEL, D_FF), D_MODEL ** -0.5),
        'conv_w': normal(ks[22], (L, CONV_WIDTH, D_FF), CONV_WIDTH ** -0.5),
        'conv_b': normal(ks[23], (L, D_FF), 0.01),
        'w_down': normal(ks[24], (L, D_FF, D_MODEL), D_FF ** -0.5),
        'norm_final_g': gain(ks[25], (D_MODEL,)),
    }


def reference(x, mem, rel_bias, norm_mix_g, w_in, cmp_pos_k, cmp_w1_k, cmp_w2_k, cmp_pos_v, cmp_w1_v, cmp_w2_v,
              w_branch_a, w_branch_b, w_out, norm_xattn_g, norm_mem_g, w_xq, w_xkv, w_xo,
              norm_ffn_g, w_gate, w_up, conv_w, conv_b, w_down, norm_final_g):
    for l in range(DEPTH):
        x = hybrid_layer(x, mem, rel_bias, norm_mix_g[l], w_in[l], cmp_pos_k[l], cmp_w1_k[l], cmp_w2_k[l],
                         cmp_pos_v[l], cmp_w1_v[l], cmp_w2_v[l], w_branch_a[l], w_branch_b[l], w_out[l],
                         norm_xattn_g[l], norm_mem_g[l], w_xq[l], w_xkv[l], w_xo[l],
                         norm_ffn_g[l], w_gate[l], w_up[l], conv_w[l], conv_b[l], w_down[l])
    return rms_norm(x, norm_final_g)
```

```python
import math
import numpy as np
import ml_dtypes
import concourse.bass as bass
import concourse.mybir as mybir
from concourse.bass_utils import run_bass_kernel_spmd

F32 = mybir.dt.float32
BF16 = mybir.dt.bfloat16
AF = mybir.ActivationFunctionType
ALU = mybir.AluOpType
AX = mybir.AxisListType

ENGS = ("pe", "act", "dve", "pool", "sp")
NEG = -30000.0
L = 8192
NO = 2176
O0 = 6016
D = 1024
DFF = 2816
NFC = 22


class Buf:
    __slots__ = ("t", "lw", "rd", "name", "psum")

    def __init__(self, t, name="", psum=False):
        self.psum = psum
        self.t = t
        self.lw = None
        self.rd = {}
        self.name = name

    def __getitem__(self, idx):
        return self.t[idx]


class Prog:
    NDMA = 8

    def __init__(self, nc):
        self.nc = nc
        self.ops = {e: [] for e in ENGS}
        self.cnt = {e: 0 for e in ENGS}
        self.sem = {e: nc.alloc_semaphore(name="s_" + e) for e in ENGS}
        self.known = {e: {} for e in ENGS}
        self.dsem, self.dcnt, self.dnext = {}, {}, {}
        for e in ("sp", "act", "pool"):
            self.dsem[e] = [nc.alloc_semaphore(name="d_%s%d" % (e, i)) for i in range(self.NDMA)]
            self.dcnt[e] = [0] * self.NDMA
            self.dnext[e] = 0
        self.semobj = {}
        for e in ENGS:
            self.semobj[("e", e)] = self.sem[e]
        for e in self.dsem:
            for i, s in enumerate(self.dsem[e]):
                self.semobj[("d", e, i)] = s
        self.ninst = 0
        self.nwaits = 0

    def _need(self, eng, key, val, waits):
        if self.known[eng].get(key, 0) >= val:
            return
        if waits.get(key, 0) < val:
            waits[key] = val

    def _emit_waits(self, eng, waits):
        for key, val in waits.items():
            self.known[eng][key] = val
            self.ops[eng].append(("w", self.semobj[key], val))
            self.nwaits += 1

    def _collect(self, eng, r, w, skip_same, waits):
        me = ("e", eng)
        for b in r:
            if b.lw is not None and not (skip_same and b.lw[0] == me):
                self._need(eng, b.lw[0], b.lw[1], waits)
            if b.psum:
                for k, v in b.rd.items():
                    if k != me:
                        self._need(eng, k, v, waits)
        for b in w:
            if b.lw is not None and not (skip_same and b.lw[0] == me):
                self._need(eng, b.lw[0], b.lw[1], waits)
            for k, v in b.rd.items():
                if not (skip_same and k == me):
                    self._need(eng, k, v, waits)

    def _record(self, ev, r, w):
        for b in r:
            if b.rd.get(ev[0], 0) < ev[1]:
                b.rd[ev[0]] = ev[1]
        for b in w:
            b.lw = ev
            b.rd = {}

    def op(self, eng, fn, r=(), w=(), skip_same=False):
        waits = {}
        self._collect(eng, r, w, skip_same, waits)
        self._emit_waits(eng, waits)
        self.cnt[eng] += 1
        self.ops[eng].append(("i", fn, self.sem[eng], 1))
        ev = (("e", eng), self.cnt[eng])
        self._record(ev, r, w)
        self.ninst += 1
        return ev

    def dma(self, eng, fn, r=(), w=()):
        i = self.dnext[eng]
        self.dnext[eng] = (i + 1) % self.NDMA
        key = ("d", eng, i)
        waits = {}
        if self.dcnt[eng][i] > 0:
            self._need(eng, key, self.dcnt[eng][i], waits)
        for b in r:
            if b.lw is not None:
                self._need(eng, b.lw[0], b.lw[1], waits)
        for b in w:
            if b.lw is not None:
                self._need(eng, b.lw[0], b.lw[1], waits)
            for k, v in b.rd.items():
                self._need(eng, k, v, waits)
        self._emit_waits(eng, waits)
        self.dcnt[eng][i] += 16
        self.ops[eng].append(("i", fn, self.dsem[eng][i], 16))
        ev = (key, self.dcnt[eng][i])
        self._record(ev, r, w)
        self.ninst += 1
        return ev

    def _all_events(self):
        evs = []
        for e in ENGS:
            if self.cnt[e] > 0:
                evs.append((("e", e), self.cnt[e]))
        for e in self.dsem:
            for i in range(self.NDMA):
                if self.dcnt[e][i] > 0:
                    evs.append((("d", e, i), self.dcnt[e][i]))
        return evs

    def barrier(self):
        evs = self._all_events()
        for e in ENGS:
            waits = {}
            for k, v in evs:
                if k != ("e", e):
                    self._need(e, k, v, waits)
            self._emit_waits(e, waits)

    def finish_waits(self, eng="sp"):
        waits = {}
        for k, v in self._all_events():
            if k != ("e", eng):
                self._need(eng, k, v, waits)
        self._emit_waits(eng, waits)

    def run(self, block):
        def play(engobj, lst):
            for it in lst:
                if it[0] == "w":
                    engobj.wait_ge(it[1], it[2])
                else:
                    it[1](engobj).then_inc(it[2], it[3])

        @block.tensor
        def _(e):
            play(e, self.ops["pe"])

        @block.scalar
        def _(e):
            play(e, self.ops["act"])

        @block.vector
        def _(e):
            play(e, self.ops["dve"])

        @block.gpsimd
        def _(e):
            play(e, self.ops["pool"])

        @block.sync
        def _(e):
            play(e, self.ops["sp"])


class Arena:
    def __init__(self, nc, nbytes):
        self.t = nc.alloc_sbuf_tensor("arena", [128, nbytes // 2], BF16)
        self.cap = nbytes
        self.off = 0
        self.peak = 0

    def alloc(self, name, shape, dtype):
        isz = 4 if dtype == F32 else 2
        n = 1
        for s in shape:
            n *= s
        nb = (n * isz + 63) // 64 * 64
        assert self.off + nb <= self.cap, ("arena overflow", name, self.off, nb, self.cap)
        ap = self.t[:, self.off // 2:(self.off + n * isz) // 2]
        if dtype == F32:
            ap = ap.bitcast(F32)
        if len(shape) == 2:
            ap = ap.rearrange("p (a b) -> p a b", a=shape[0])
        elif len(shape) == 3:
            ap = ap.rearrange("p (a b c) -> p a b c", a=shape[0], b=shape[1])
        elif len(shape) == 4:
            ap = ap.rearrange("p (a b c d) -> p a b c d", a=shape[0], b=shape[1], c=shape[2])
        self.off += nb
        self.peak = max(self.peak, self.off)
        return Buf(ap, name)


def bcast(ap, pos, n):
    lst = [list(x) for x in ap.ap]
    lst.insert(pos, [0, n])
    return bass.AP(ap.tensor, ap.offset, lst)


def t5_bucket_np(d):
    n = np.maximum(d, 0)
    nf = np.maximum(n, 1).astype(np.float32)
    large = 16 + (np.log(nf / np.float32(16)) / np.float32(math.log(128 / 16)) * np.float32(16)).astype(np.int32)
    large = np.minimum(large, 31)
    return np.where(n < 16, n, large)


def host_tables(rel_bias, c):
    pad = 2048 * (3 - c)
    T = {}
    k = np.arange(128)[:, None]
    j = np.arange(512)[None, :]
    tb = np.empty((8, 128, 5, 512), np.float32)
    for oi in range(5):
        d = j - k - (oi - 1) * 128
        bk = t5_bucket_np(d)
        for h in range(8):
            tb[h, :, oi, :] = np.where(d >= 0, rel_bias[bk, h], np.float32(NEG))
    T["tb_moba"] = tb
    T["c31"] = np.broadcast_to(rel_bias[31][None, :], (128, 16)).astype(np.float32).copy()
    am = np.zeros((128, 17, 32), np.float32)
    pv = np.zeros((128, 17, 32), np.float32)
    ow = np.zeros((128, 17, 32), np.float32)
    padb = pad // 256
    for qi in range(17):
        cur = (47 + qi) // 2
        n = np.arange(32)
        val = (n >= padb) & (n < cur)
        am[:, qi, :] = np.where(val, 0.0, NEG)[None, :]
        pv[:, qi, :] = val.astype(np.float32)[None, :]
        ow[:, qi, :] = (n == cur).astype(np.float32)[None, :]
    T["mg_add"], T["mg_pv"], T["mg_own"] = am, pv, ow
    kk = np.arange(L)
    T["e_moba"] = (kk[None, :] // 256 == np.arange(32)[:, None]).astype(ml_dtypes.bfloat16)
    T["e_sel"] = ((kk[None, :] // 64) % 64 == np.arange(64)[:, None]).astype(ml_dtypes.bfloat16)
    T["kinv"] = (kk < pad).astype(ml_dtypes.bfloat16)[None, :]
    j = np.arange(128)[None, :]
    ts = np.empty((2, 128, 2, 4, 128), np.float32)
    tw = np.empty((2, 128, 5, 4, 128), np.float32)
    for g in range(2):
        for hh in range(4):
            h = 8 + 4 * g + hh
            for oi, off in enumerate((-128, 0)):
                d = j - k - off
                ts[g, :, oi, hh, :] = np.where(d >= 0, rel_bias[t5_bucket_np(d), h], np.float32(NEG))
            for oi, off in enumerate((-512, -384, -256, -128, 0)):
                d = j - k - off
                tw[g, :, oi, hh, :] = np.where((d >= 0) & (d < 512), rel_bias[t5_bucket_np(d), h], np.float32(NEG))
    T["tb_sel"], T["tb_win"] = ts, tw
    sa = np.zeros((128, 17, 128), np.float32)
    sv = np.zeros((128, 17, 128), np.float32)
    jl = np.arange(128)[None, :]
    for qi in range(17):
        tg = 128 * (47 + qi) + np.arange(128)[:, None] - pad
        curg = tg // 64
        jg = jl - pad // 64
        valid = (jg >= 0) & (jg <= curg) & (tg >= 0)
        forced = valid & ((jg == 0) | (jg == curg) | (jg == curg - 1))
        sa[:, qi, :] = np.where(forced, 1e4, np.where(valid, 0.0, -1e4))
        sv[:, qi, :] = valid.astype(np.float32)
    T["sel_add"], T["sel_valid"] = sa, sv
    cm = np.zeros((17, 128, 512), np.float32)
    n = np.arange(512)[None, :]
    for qi in range(17):
        tg = 128 * (47 + qi) + np.arange(128)[:, None] - pad
        ng = n - pad // 16
        vis = (ng >= 0) & (n <= 510) & (16 * ng + 31 <= tg)
        cm[qi] = np.where(vis, 0.0, NEG)
    T["cm_tm"] = cm.astype(ml_dtypes.bfloat16)
    T["cm_T"] = np.ascontiguousarray(cm.reshape(17, 128, 4, 128).transpose(0, 3, 2, 1)).astype(ml_dtypes.bfloat16)
    T["ident"] = np.eye(128, dtype=np.float32)
    T["flag"] = np.full((128, 1), 0.0 if c == 0 else 1.0, np.float32)
    return T


TABLE_SPECS = [
    ("tb_moba", [8, 128, 5, 512], F32), ("c31", [128, 16], F32),
    ("mg_add", [128, 17, 32], F32), ("mg_pv", [128, 17, 32], F32), ("mg_own", [128, 17, 32], F32),
    ("e_moba", [32, L], BF16), ("e_sel", [64, L], BF16), ("kinv", [1, L], BF16),
    ("tb_sel", [2, 128, 2, 4, 128], F32), ("tb_win", [2, 128, 5, 4, 128], F32),
    ("sel_add", [128, 17, 128], F32), ("sel_valid", [128, 17, 128], F32),
    ("cm_tm", [17, 128, 512], BF16), ("cm_T", [17, 128, 4, 128], BF16),
    ("ident", [128, 128], F32), ("flag", [128, 1], F32),
]

WEIGHT_SPECS = [
    ("mem", [256, D]), ("norm_mix_g", [1, D]), ("w_in", [D, 4888]),
    ("cmp_pos_k", [32, 64]), ("cmp_w1_k", [2048, 256]), ("cmp_w2_k", [256, 64]),
    ("cmp_pos_v", [32, 64]), ("cmp_w1_v", [2048, 256]), ("cmp_w2_v", [256, 64]),
    ("w_branch_a", [512, D]), ("w_branch_b", [512, D]), ("w_out", [D, D]),
    ("norm_xattn_g", [1, D]), ("norm_mem_g", [1, D]), ("w_xq", [D, 512]), ("w_xkv", [D, D]),
    ("w_xo", [512, D]), ("norm_ffn_g", [1, D]), ("w_gate", [D, DFF]), ("w_up", [D, DFF]),
    ("conv_w", [3, DFF]), ("conv_b", [1, DFF]), ("w_down", [DFF, D]), ("norm_final_g", [1, D]),
]

C_MQ, C_MK, C_MV, C_NQ = 0, 512, 1024, 1536
C_NKC, C_NVC, C_NKS, C_NVS, C_NKW, C_NVW = 2048, 2176, 2304, 2432, 2560, 2688
C_NG, C_GA, C_GB = 2816, 2840, 3864
KG_COLS = [C_MK, C_MK + 128, C_MK + 256, C_MK + 384, C_NKC, C_NKS, C_NKW, C_NVC]


def build(upto=99, debug=False, stop=None):
    nc = bass.Bass("TRN2", target_bir_lowering=False)
    P = Prog(nc)
    A = Arena(nc, 206 * 1024)
    IN = {}

    def din(name, shape, dt=F32):
        IN[name] = nc.dram_tensor(name, list(shape), dt, kind="ExternalInput").ap()

    din("xl", [L, D])
    for n_, s_ in WEIGHT_SPECS:
        din(n_, s_)
    for n_, s_, d_ in TABLE_SPECS:
        din(n_, s_, d_)
    out = nc.dram_tensor("out", [2048, D], F32, kind="ExternalOutput").ap()
    skind = "ExternalOutput" if debug else "Internal"

    def scr(name, shape, dt):
        return nc.dram_tensor(name, list(shape), dt, kind=skind).ap()

    KT = scr("KT", [8, 128, L], BF16)
    VT = scr("VT", [12, 128, 64 * 65], BF16)
    QS = scr("QS", [4, 128, NO], BF16)
    QF = scr("QF", [4, 128, NO], F32)
    NQ = scr("NQ", [4, 128, NO], BF16)
    GAB = scr("GAB", [16, 128, NO], BF16)
    NG = scr("NG", [24, NO], F32)
    OA = scr("OA", [8, 64, NO], BF16)
    OB = scr("OB", [8, 64, NO], BF16)
    WGs = scr("WGs", [NFC, 128, 8, 128], BF16)
    WUs = scr("WUs", [NFC, 128, 8, 128], BF16)
    WDs = scr("WDs", [NFC, 128, D], BF16)
    WAs = scr("WAs", [512, D], BF16)
    WBs = scr("WBs", [512, D], BF16)
    WOs = scr("WOs", [D, D], BF16)
    WXQs = scr("WXQs", [D, 512], BF16)
    WXKVs = scr("WXKVs", [D, D], BF16)
    WXOs = scr("WXOs", [512, D], BF16)

    psF = [Buf(nc.alloc_psum_tensor("psF%d" % i, [128, 512], F32), "psF%d" % i, psum=True) for i in range(6)]
    psT = [Buf(nc.alloc_psum_tensor("psT%d" % i, [128, 1024], BF16), "psT%d" % i, psum=True) for i in range(2)]

    def MM(out_, lhsT, rhs, start, stop, r, w):
        P.op("pe", lambda e: e.matmul(out_, lhsT=lhsT, rhs=rhs, start=start, stop=stop), r=r, w=w, skip_same=True)

    def TR(out_, in_, ident, r, w):
        P.op("pe", lambda e: e.transpose(out=out_, in_=in_, identity=ident), r=r, w=w, skip_same=True)

    def ACT(out_, in_, func, r, w, **kw):
        P.op("act", lambda e: e.activation(out=out_, in_=in_, func=func, **kw), r=r, w=w)

    def CP(eng, out_, in_, r, w):
        if eng == "act":
            P.op("act", lambda e: e.activation(out=out_, in_=in_, func=AF.Copy), r=r, w=w)
        else:
            P.op(eng, lambda e: e.tensor_copy(out=out_, in_=in_), r=r, w=w)

    def TT(out_, in0, in1, op, r, w, eng="dve"):
        P.op(eng, lambda e: e.tensor_tensor(out=out_, in0=in0, in1=in1, op=op), r=r, w=w)

    def TS(out_, in0, s1, s2, op0, op1, r, w, eng="dve"):
        if op1 is None:
            P.op(eng, lambda e: e.tensor_scalar(out=out_, in0=in0, scalar1=s1, scalar2=None, op0=op0), r=r, w=w)
        else:
            P.op(eng, lambda e: e.tensor_scalar(out=out_, in0=in0, scalar1=s1, scalar2=s2, op0=op0, op1=op1), r=r, w=w)

    def STT(out_, in0, scalar, in1, op0, op1, r, w, eng="dve"):
        P.op(eng, lambda e: e.scalar_tensor_tensor(out=out_, in0=in0, scalar=scalar, in1=in1, op0=op0, op1=op1), r=r, w=w)

    def MEMSET(eng, ap, val, w):
        P.op(eng, lambda e: e.memset(ap, val), w=w)

    def DMA(q, out_, in_, r, w, slow=False):
        if slow:
            P.dma(q, lambda e: e.dma_start(out=out_, in_=in_, allow_slow_non_contiguous=True), r=r, w=w)
        else:
            P.dma(q, lambda e: e.dma_start(out=out_, in_=in_), r=r, w=w)

    ident_f = A.alloc("ident_f", [128], F32)
    ident_b = A.alloc("ident_b", [128], BF16)
    ones_f = A.alloc("ones_f", [128], F32)
    ones_b = A.alloc("ones_b", [128], BF16)
    c31t = A.alloc("c31t", [16], F32)
    kmean = A.alloc("kmean", [4, 32], F32)
    flag = A.alloc("flag", [1], F32)
    DMA("sp", ident_f[:], IN["ident"][:, :], [], [ident_f])
    CP("dve", ident_b[:], ident_f[:], [ident_f], [ident_b])
    MEMSET("pool", ones_f[:], 1.0, [ones_f])
    MEMSET("pool", ones_b[:], 1.0, [ones_b])
    DMA("sp", c31t[:], IN["c31"][:, :], [], [c31t])
    DMA("sp", flag[:], IN["flag"][:, :], [], [flag])
    base_mark = A.off

    def norm_block(xs_list, gB, junk, stat, xn_list, hT_buf, ncols, gainT=None):
        for _ in norm_gen(xs_list, gB, junk, stat, xn_list, hT_buf, ncols, gainT):
            pass

    def norm_gen(xs_list, gB, junk, stat, xn_list, hT_buf, ncols, gainT=None):
        ns = len(xs_list)
        for s, (xb, xap) in enumerate(xs_list):
            ACT(junk[:], xap, AF.Square, [xb], [junk, stat], accum_out=stat[:, s:s + 1])
        ACT(stat[:, 4:4 + ns], stat[:, 0:ns], AF.Sqrt, [stat], [stat], scale=1.0 / D, bias=EPS_T[:, 0:1])
        P.op("dve", lambda e: e.reciprocal(out=stat[:, 8:8 + ns], in_=stat[:, 4:4 + ns]), r=[stat], w=[stat])
        yield
        nx = len(xn_list)
        for s, (xb, xap) in enumerate(xs_list):
            xn = xn_list[s % nx]
            if gainT is None:
                STT(xn[:], xap, stat[:, 8 + s:9 + s], gB[:], ALU.mult, ALU.mult, [xb, stat, gB], [xn])
            else:
                TS(xn[:], xap, stat[:, 8 + s:9 + s], None, ALU.mult, None, [xb, stat], [xn])
        for s, (xb, xap) in enumerate(xs_list):
            if s > 0:
                yield
            xn = xn_list[s % nx]
            pt = psT[s % 2]
            for k in range(8):
                TR(pt[:, k * 128:(k + 1) * 128], xn[:, k * 128:(k + 1) * 128], ident_b[:], [xn, ident_b], [pt])
            src = pt[:, :].rearrange("p (k t) -> p k t", k=8)
            if gainT is None:
                CP("act" if s % 2 == 0 else "dve", hT_buf[:, :, s * 128:(s + 1) * 128], src, [pt], [hT_buf])
            else:
                TT(hT_buf[:, :, s * 128:(s + 1) * 128], src, bcast(gainT[:, 0:8], 2, 128), ALU.mult, [pt, gainT], [hT_buf])

    EPS_T = A.alloc("eps_t", [1], F32)
    MEMSET("pool", EPS_T[:], 1e-6, [EPS_T])
    EPS30 = A.alloc("eps30", [1], F32)
    MEMSET("pool", EPS30[:], 1e-30, [EPS30])
    base_mark = A.off

    Wb = A.alloc("Wb", [8, 4888], BF16)
    WbQ = Buf(Wb.t, "WbQ")
    wst = [A.alloc("wst%d" % i, [1036], F32) for i in range(6)]
    gB = A.alloc("gB", [D], F32)
    xst = [A.alloc("xst%d" % i, [D], F32) for i in range(8)]
    junk = A.alloc("junk", [D], BF16)
    xn_l = [A.alloc("xn%d" % i, [D], BF16) for i in range(4)]
    hT = [A.alloc("hT%d" % i, [8, 512], BF16) for i in range(2)]
    stK = [A.alloc("stK%d" % i, [8, 512], BF16) for i in range(2)]
    stV = [A.alloc("stV%d" % i, [12, 4, 65], BF16) for i in range(2)]
    stQ = [A.alloc("stQ%d" % i, [512], BF16) for i in range(4)]
    stQF = [A.alloc("stQF%d" % i, [512], F32) for i in range(2)]
    stat = A.alloc("stat", [12], F32)

    DMA("sp", gB[:], IN["norm_mix_g"][0:1, :].broadcast_to([128, D]), [], [gB])
    for i in range(2):
        MEMSET("pool", stV[i][:, :, :, 64:65], 1.0, [stV[i]])
    wi = IN["w_in"]
    wchunks = [(k, c0, c1) for (c0, c1) in ((512, 1536), (2048, 2816)) for k in range(8)]
    wchunks += [(k, c0, c1) for (c0, c1) in ((0, 512), (1536, 2048), (2816, 3852), (3852, 4888)) for k in range(8)]
    wcount = [0]
    wpending = []

    def wcast():
        while wpending:
            i, k, c0, c1 = wpending.pop(0)
            ws = wst[i % 6]
            CP(("dve", "act", "pool")[i % 3], Wb[:, k, c0:c1], ws[:, 0:c1 - c0], [ws], [Wb if i < 16 else WbQ])

    def wload(n, cast_now=False):
        for _ in range(n):
            if wcount[0] >= len(wchunks):
                return
            k, c0, c1 = wchunks[wcount[0]]
            ws = wst[wcount[0] % 6]
            DMA("sp", ws[:, 0:c1 - c0], wi[k * 128:(k + 1) * 128, c0:c1], [], [ws])
            wpending.append((wcount[0], k, c0, c1))
            wcount[0] += 1
            if cast_now:
                wcast()

    wload(16, cast_now=True)

    if stop == 'a':
        return finish(nc, P, A)
    xl = IN["xl"]

    def load_x(blk):
        for s in range(4):
            ti = blk * 4 + s
            xs = xst[ti % 8]
            DMA("sp", xs[:], xl[ti * 128:(ti + 1) * 128, :], [], [xs])

    load_x(0)
    load_x(1)
    qcnt = [0]
    KTv = KT.rearrange("g p t -> p g t")
    VTv = VT.rearrange("h p f -> p h f")
    statA = [stat, A.alloc("statb", [12], F32)]

    def ngen(blk):
        return norm_gen([(xst[(blk * 4 + s) % 8], xst[(blk * 4 + s) % 8][:]) for s in range(4)],
                        gB, junk, statA[blk % 2], xn_l, hT[blk % 2], 512)

    for _ in ngen(0):
        pass
    for blk in range(16):
        if blk + 2 < 16:
            load_x(blk + 2)
        elif blk == 0:
            pass
        hb = hT[blk % 2]
        nxt_norm = ngen(blk + 1) if blk + 1 < 16 else None
        wcast()
        wload(3)
        if stop == 'b':
            continue
        sk = stK[blk % 2]
        for kg in range(8):
            ps = psF[kg % 2]
            c0 = KG_COLS[kg]
            for k in range(8):
                MM(ps[:, :], Wb[:, k, c0:c0 + 128], hb[:, k, :], k == 0, k == 7, [Wb, hb], [ps])
            CP("act", sk[:, kg, :], ps[:, :], [ps], [sk])
            if nxt_norm is not None and kg in (1, 3, 5, 7):
                next(nxt_norm, None)
            if kg < 4:
                P.op("dve", lambda e, ps=ps, kg=kg, blk=blk: e.tensor_reduce(
                    out=kmean[:, kg, 2 * blk:2 * blk + 2], in_=ps[:, :].rearrange("p (a b) -> p a b", a=2),
                    axis=AX.X, op=ALU.add), r=[ps], w=[kmean])
        DMA("pool", KTv[:, :, blk * 512:(blk + 1) * 512], sk[:], [sk], [])
        if stop == 'c':
            continue
        sv = stV[blk % 2]
        for s in range(4):
            ps = psF[2 + s % 2]
            for k in range(8):
                MM(ps[:, :], hb[:, k, s * 128:(s + 1) * 128], Wb[:, k, C_MV:C_MV + 512], k == 0, k == 7, [Wb, hb], [ps])
            CP("dve", sv[:, 0:8, s, 0:64], ps[:, :].rearrange("p (h e) -> p h e", h=8), [ps], [sv])
            ps2 = psF[4]
            for k in range(8):
                MM(ps2[:, 0:128], hb[:, k, s * 128:(s + 1) * 128], Wb[:, k, C_NVS:C_NVS + 128], k == 0, k == 7, [Wb, hb], [ps2])
            for k in range(8):
                MM(ps2[:, 128:256], hb[:, k, s * 128:(s + 1) * 128], Wb[:, k, C_NVW:C_NVW + 128], k == 0, k == 7, [Wb, hb], [ps2])
            CP("dve", sv[:, 8:12, s, 0:64], ps2[:, 0:256].rearrange("p (h e) -> p h e", h=4), [ps2], [sv])
        DMA("pool", VTv[:, :, blk * 260:(blk + 1) * 260], sv[:, :, :, :].rearrange("p h s e -> p h (s e)"), [sv], [])
        if nxt_norm is not None:
            for _ in nxt_norm:
                pass
        if stop == 'd':
            continue
        if blk >= 11:
            t0, n = (384, 128) if blk == 11 else (0, 512)
            o0 = blk * 512 + t0 - O0
            def qgroup(c0, m, kind, dst):
                i = qcnt[0]
                qcnt[0] += 1
                ps = psF[i % 2]
                for k in range(8):
                    MM(ps[0:m, 0:n], Wb[:, k, c0:c0 + m], hb[:, k, t0:t0 + n], k == 0, k == 7, [WbQ, hb], [ps])
                if kind == "q":
                    sq = stQ[i % 4]
                    ACT(sq[0:m, 0:n], ps[0:m, 0:n], AF.Copy, [ps], [sq], scale=0.125)
                    DMA("sp", dst[0], sq[0:m, 0:n], [sq], [])
                    if dst[1] is not None:
                        sf = stQF[i % 2]
                        CP("dve", sf[0:m, 0:n], ps[0:m, 0:n], [ps], [sf])
                        DMA("sp", dst[1], sf[0:m, 0:n], [sf], [])
                elif kind == "sig":
                    sq = stQ[i % 4]
                    ACT(sq[0:m, 0:n], ps[0:m, 0:n], AF.Sigmoid, [ps], [sq])
                    DMA("sp", dst[0], sq[0:m, 0:n], [sq], [])
                else:
                    sf = stQF[i % 2]
                    ACT(sf[0:m, 0:n], ps[0:m, 0:n], AF.Sigmoid, [ps], [sf])
                    DMA("sp", dst[0], sf[0:m, 0:n], [sf], [])
            for g in range(4):
                qgroup(C_MQ + 128 * g, 128, "q", (QS[g, :, o0:o0 + n], QF[g, :, o0:o0 + n]))
            for g in range(4):
                qgroup(C_NQ + 128 * g, 128, "q", (NQ[g, :, o0:o0 + n], None))
            for g in range(8):
                qgroup(C_GA + 128 * g, 128, "sig", (GAB[g, :, o0:o0 + n],))
            for g in range(8):
                qgroup(C_GB + 128 * g, 128, "sig", (GAB[8 + g, :, o0:o0 + n],))
            qgroup(C_NG, 24, "sigf", (NG[:, o0:o0 + n],))
    P.barrier()
    A.off = base_mark
    if upto <= 1:
        return finish(nc, P, A)

    psS = [psF[0], psF[1]] + [Buf(psT[i][:, :].bitcast(F32), "psS%d" % (2 + i), psum=True) for i in range(2)]
    SKEW = 3
    KE = [A.alloc("KE%d" % i, [L], BF16) for i in range(2)]
    VP = [A.alloc("VP%d" % i, [64 * 65], BF16) for i in range(2)]
    TBm = [A.alloc("TBm%d" % i, [5, 512], F32) for i in range(2)]
    mg_add = A.alloc("mg_add", [17, 32], F32)
    mg_pv = A.alloc("mg_pv", [17, 32], F32)
    mg_own = A.alloc("mg_own", [17, 32], F32)
    QN = [A.alloc("QN%d" % i, [512], BF16) for i in range(2)]
    QFt = [A.alloc("QFt%d" % i, [512], F32) for i in range(2)]
    nmx = [A.alloc("nmx%d" % i, [96], F32) for i in range(2)]
    gmb = [A.alloc("gmb%d" % i, [32], F32) for i in range(2)]
    alb = [A.alloc("alb%d" % i, [32], F32) for i in range(2)]
    m8 = [A.alloc("m8%d" % i, [8], F32) for i in range(2)]
    sst = [A.alloc("sst%d" % i, [512], F32) for i in range(4)]
    ptb = [A.alloc("ptb%d" % i, [512], BF16) for i in range(6)]
    osb = [A.alloc("osb%d" % i, [512], F32) for i in range(3)]
    oab = [A.alloc("oab%d" % i, [512], BF16) for i in range(3)]
    pcf = [A.alloc("pcf%d" % i, [DFF], F32) for i in range(2)]
    pcb = [A.alloc("pcb%d" % i, [DFF], BF16) for i in range(2)]

    for i in range(2):
        DMA("sp", KE[i][64:96, :], IN["e_moba"][:, :], [], [KE[i]])
        MEMSET("pool", nmx[i][:], 0.0, [nmx[i]])
    DMA("sp", mg_add[:], IN["mg_add"][:, :, :], [], [mg_add])
    DMA("sp", mg_pv[:], IN["mg_pv"][:, :, :], [], [mg_pv])
    DMA("sp", mg_own[:], IN["mg_own"][:, :, :], [], [mg_own])
    TS(kmean[:], kmean[:], 1.0 / 256.0, None, ALU.mult, None, [kmean], [kmean])

    def precast_steps():
        i = 0
        for W, Ws in ((IN["w_gate"], WGs), (IN["w_up"], WUs)):
            Wv = Ws.rearrange("c p f n -> p f c n")
            for f in range(8):
                pf, pb_ = pcf[i % 2], pcb[i % 2]
                DMA("pool", pf[:], W[f * 128:(f + 1) * 128, :], [], [pf])
                CP("pool", pb_[:], pf[:], [pf], [pb_])
                DMA("pool", Wv[:, f, :, :], pb_[:].rearrange("p (c n) -> p c n", c=NFC), [pb_], [])
                i += 1
                yield
        for c in range(NFC):
            pf, pb_ = pcf[i % 2], pcb[i % 2]
            DMA("pool", pf[:, 0:D], IN["w_down"][c * 128:(c + 1) * 128, :], [], [pf])
            CP("pool", pb_[:, 0:D], pf[:, 0:D], [pf], [pb_])
            DMA("pool", WDs[c, :, :], pb_[:, 0:D], [pb_], [])
            i += 1
            yield
        for (src, dst, R, C) in ((IN["w_branch_a"], WAs, 512, D), (IN["w_branch_b"], WBs, 512, D), (IN["w_out"], WOs, D, D),
                                 (IN["w_xq"], WXQs, D, 512), (IN["w_xkv"], WXKVs, D, D), (IN["w_xo"], WXOs, 512, D)):
            for r0 in range(0, R, 128):
                pf, pb_ = pcf[i % 2], pcb[i % 2]
                DMA("pool", pf[:, 0:C], src[r0:r0 + 128, :], [], [pf])
                CP("pool", pb_[:, 0:C], pf[:, 0:C], [pf], [pb_])
                DMA("pool", dst[r0:r0 + 128, :], pb_[:, 0:C], [pb_], [])
                i += 1
                yield

    pc_gen = precast_steps()

    def pc_step(n=1):
        for _ in range(n):
            try:
                next(pc_gen)
            except StopIteration:
                return

    def moba_load(h):
        g, pb = h // 2, 64 * (h % 2)
        hb = h % 2
        q_ = "sp" if h < 2 else "pool"
        DMA(q_, KE[hb][0:64, :], KT[g, pb:pb + 64, :], [], [KE[hb]])
        DMA(q_, VP[hb][:], VT[h, :, :], [], [VP[hb]])
        DMA(q_, TBm[hb][:], IN["tb_moba"][h, :, :, :], [], [TBm[hb]])

    CBS = [(0, 1), (1, 4), (5, 4), (9, 4), (13, 4)]
    ucnt = [0]
    pend = []

    late = []

    def run_pend(keep=0):
        while len(pend) > keep:
            pend.pop(0)()
        for it in late:
            it[0] -= 1
        i = 0
        while i < len(late):
            if late[i][0] <= 0:
                late.pop(i)[1]()
            else:
                i += 1

    def flush_late():
        while late:
            late.pop(0)[1]()

    def moba_prep(h, cbi, qi0, nt):
        g, pb = h // 2, 64 * (h % 2)
        N = 128 * nt
        o0 = 128 * qi0
        QNb, QFb = QN[cbi % 2], QFt[cbi % 2]
        DMA("sp", QNb[0:64, 0:N], QS[g, pb:pb + 64, o0:o0 + N], [], [QNb])
        DMA("sp", QFb[pb:pb + 64, 0:N], QF[g, pb:pb + 64, o0:o0 + N], [], [QFb])
        yield
        for j in range(nt):
            qi = qi0 + j
            x2 = (cbi * 4 + j) % 2
            pg = psF[4]
            MM(pg[:, 0:32], QFb[pb:pb + 64, j * 128:(j + 1) * 128], kmean[pb:pb + 64, g, :], True, True, [QFb, kmean], [pg])
            TT(gmb[x2][:], pg[:, 0:32], mg_add[:, qi, :], ALU.add, [pg, mg_add], [gmb[x2]])
            P.op("dve", lambda e, a=m8[x2], b=gmb[x2]: e.max(out=a[:], in_=b[:]), r=[gmb[x2]], w=[m8[x2]])
            STT(alb[x2][:], gmb[x2][:], m8[x2][:, 2:3], mg_pv[:, qi, :], ALU.is_ge, ALU.mult, [gmb[x2], m8[x2], mg_pv], [alb[x2]])
            TT(alb[x2][:], alb[x2][:], mg_own[:, qi, :], ALU.add, [alb[x2], mg_own], [alb[x2]])
            TS(nmx[x2][:, 64:96], alb[x2][:], 30000.0, -30000.0, ALU.mult, ALU.add, [alb[x2]], [nmx[x2]])
            yield
            yield
            pt5 = psF[5]
            TR(pt5[0:96, 0:128], nmx[x2][:, 0:96], ident_f[:], [nmx[x2], ident_f], [pt5])
            CP("dve", QNb[64:96, j * 128:(j + 1) * 128], pt5[64:96, 0:128], [pt5], [QNb])
            yield

    def moba_final(h, cbi, qi0, nt, po):
        N = 128 * nt
        o0 = 128 * qi0
        ob, oa = osb[cbi % 3], oab[cbi % 3]

        def part1():
            CP("dve", ob[0:65, 0:N], po[0:65, 0:N], [po], [ob])
            TS(ob[64:65, 0:N], ob[64:65, 0:N], 1e-30, None, ALU.max, None, [ob], [ob])
            P.op("dve", lambda e, ob=ob, N=N: e.reciprocal(out=ob[64:65, 0:N], in_=ob[64:65, 0:N]), r=[ob], w=[ob])

        def part2():
            pbc = psF[5]
            MM(pbc[0:64, 0:N], ones_f[64:65, 0:64], ob[64:65, 0:N], True, True, [ones_f, ob], [pbc])
            TT(oa[0:64, 0:N], ob[0:64, 0:N], pbc[0:64, 0:N], ALU.mult, [ob, pbc], [oa])
            DMA("sp", OA[h, :, o0:o0 + N], oa[0:64, 0:N], [oa], [])
        return part1, part2

    moba_list = [(h, qi0, nt) for h in range(8) for (qi0, nt) in CBS] if upto >= 2 else []
    if upto >= 2:
        moba_load(0)
        moba_load(1)
        prep = moba_prep(moba_list[0][0], 0, moba_list[0][1], moba_list[0][2])
        for _ in prep:
            pass
    for cbi, (h, qi0, nt) in enumerate(moba_list):
        hb = h % 2
        if qi0 == 0 and h >= 1 and h + 1 < 8:
            run_pend()
            moba_load(h + 1)
        N = 128 * nt
        QNb = QN[cbi % 2]
        nxt = None
        if cbi + 1 < len(moba_list):
            h2, q2, n2 = moba_list[cbi + 1]
            nxt = moba_prep(h2, cbi + 1, q2, n2)
        pc_step(2)
        po = psF[2 + cbi % 2]
        nkt = 47 + qi0 + nt
        for kt in range(nkt):
            u = ucnt[0]
            ucnt[0] += 1
            ps = psS[u % 4]
            pt_ = ptb[u % 6]
            MM(ps[:, 0:N], KE[hb][0:96, kt * 128:(kt + 1) * 128], QNb[0:96, 0:N], True, True, [KE[hb], QNb], [ps])
            offi = kt - (47 + qi0) + 1
            if offi >= 0:
                ss = sst[u % 4]
                TT(ss[:, 0:N], ps[:, 0:N], TBm[hb][:, offi, 0:N], ALU.add, [ps, TBm[hb]], [ss])
                ACT(pt_[:, 0:N], ss[:, 0:N], AF.Exp, [ss], [pt_])
            else:
                ACT(pt_[:, 0:N], ps[:, 0:N], AF.Exp, [ps, c31t], [pt_], bias=c31t[:, h:h + 1])
            run_pend(SKEW - 1)
            pend.append(lambda po=po, hb=hb, kt=kt, pt_=pt_, N=N, nkt=nkt: MM(
                po[0:65, 0:N], VP[hb][:, kt * 65:(kt + 1) * 65], pt_[:, 0:N], kt == 0, kt == nkt - 1, [VP[hb], pt_], [po]))
            if nxt is not None and kt >= 2 and kt % 2 == 0:
                next(nxt, None)
        if nxt is not None:
            for _ in nxt:
                pass
        p1_, p2_ = moba_final(h, cbi, qi0, nt, po)
        late.append([SKEW + 1, p1_])
        late.append([SKEW + 22, p2_])
    run_pend()
    flush_late()
    if upto >= 2:
        for _ in range(80):
            pc_step(1)
    P.barrier()
    A.off = base_mark
    if upto <= 2:
        return finish(nc, P, A)

    W1 = A.alloc("W1", [32, 256], BF16)
    w1st = A.alloc("w1st", [4, 256], F32)
    W2 = [A.alloc("W2%d" % i, [2, 64], BF16) for i in range(2)]
    w2st = A.alloc("w2st", [2, 64], F32)
    posT = A.alloc("posT", [32], BF16)
    posst = A.alloc("posst", [32], F32)
    hbias = [A.alloc("hbias%d" % i, [2], F32) for i in range(2)]
    srcT = A.alloc("srcT", [L], BF16)
    srcD = A.alloc("srcD", [16, 512], BF16)
    hid = [A.alloc("hid%d" % i, [2, 512], BF16) for i in range(2)]
    KC = A.alloc("KC", [512], BF16)
    VCp = A.alloc("VCp", [4, 65], BF16)
    KS = A.alloc("KS", [L], BF16)
    KW = A.alloc("KW", [L], BF16)
    VS = A.alloc("VS", [64 * 65], BF16)
    VW = A.alloc("VW", [64 * 65], BF16)
    QG = A.alloc("QG", [4, NO], BF16)
    TBs = A.alloc("TBs", [2, 4, 128], F32)
    TBw = A.alloc("TBw", [5, 4, 128], F32)
    sa_t = [A.alloc("sa%d" % i, [128], F32) for i in range(2)]
    sv_t = [A.alloc("sv%d" % i, [128], F32) for i in range(2)]
    cmt_t = [A.alloc("cmt%d" % i, [512], BF16) for i in range(2)]
    cmT_t = [A.alloc("cmT%d" % i, [4, 128], BF16) for i in range(2)]
    TBsb = A.alloc("TBsb", [2, 4, 128], BF16)
    TBwb = A.alloc("TBwb", [5, 4, 128], BF16)
    GRt = [A.alloc("GRt%d" % i, [3, 4, 128], F32) for i in range(2)]
    pgx = A.alloc("pgx", [516], F32)
    pe_t = [A.alloc("pe%d" % i, [512], F32) for i in range(2)]
    rs_t = A.alloc("rs", [8], F32)
    impb = A.alloc("imp", [128], F32)
    t1b = A.alloc("t1b", [128], F32)
    t2b = A.alloc("t2b", [128], F32)
    wkb = A.alloc("wkb", [128], F32)
    selb = A.alloc("selb", [128], F32)
    m16 = A.alloc("m16", [16], F32)
    nmc = A.alloc("nmc", [192], F32)
    QNsA = [A.alloc("QNsA%d" % i, [4, 128], BF16) for i in range(2)]
    QNsB = [A.alloc("QNsB%d" % i, [4, 128], BF16) for i in range(2)]
    QNw = [A.alloc("QNw%d" % i, [4, 128], BF16) for i in range(2)]
    ptc = [A.alloc("ptc%d" % i, [512], BF16) for i in range(5)]
    obr = [A.alloc("obr%d" % i, [512], F32) for i in range(5)]
    oacc = A.alloc("oacc", [512], F32)
    otmp = A.alloc("otmp", [512], F32)
    obb = [A.alloc("obb%d" % i, [512], BF16) for i in range(2)]

    MEMSET("pool", pgx[:], 0.0, [pgx])
    MEMSET("pool", nmc[:], 0.0, [nmc])
    MEMSET("pool", VCp[:, :, 64:65], 1.0, [VCp])
    for i in range(2):
        MEMSET("pool", hid[i][:], 0.0, [hid[i]])
        MEMSET("pool", QNw[i][64:65, :, :], NEG, [QNw[i]])
    for i, (w1n, w2n, pn) in enumerate((("cmp_w1_k", "cmp_w2_k", "cmp_pos_k"), ("cmp_w1_v", "cmp_w2_v", "cmp_pos_v"))):
        w1v = IN[w1n].rearrange("(j d) n -> d j n", d=64)
        p0 = 64 * i
        for jh in range(8):
            DMA("sp", w1st[p0:p0 + 64, :, :], w1v[:, jh * 4:(jh + 1) * 4, :], [], [w1st])
            CP("dve", W1[p0:p0 + 64, jh * 4:(jh + 1) * 4, :], w1st[p0:p0 + 64, :, :], [w1st], [W1])
        DMA("sp", w2st[:], IN[w2n].rearrange("(hf p) e -> p hf e", p=128), [], [w2st])
        CP("dve", W2[i][:], w2st[:], [w2st], [W2[i]])
        DMA("sp", posst[p0:p0 + 64, :], IN[pn].rearrange("j d -> d j"), [], [posst], slow=True)
        CP("dve", posT[p0:p0 + 64, :], posst[p0:p0 + 64, :], [posst], [posT])
        for half in range(2):
            pb_ = psF[4]
            for j in range(32):
                MM(pb_[:, 0:1], W1[p0:p0 + 64, j, half * 128:(half + 1) * 128], posT[p0:p0 + 64, j:j + 1], j == 0, j == 31, [W1, posT], [pb_])
            CP("dve", hbias[i][:, half:half + 1], pb_[:, 0:1], [pb_], [hbias[i]])

    ucnt = [0]
    bcnt = [0]
    pend = []

    late = []

    def run_pend(keep=0):
        while len(pend) > keep:
            pend.pop(0)()
        for it in late:
            it[0] -= 1
        i = 0
        while i < len(late):
            if late[i][0] <= 0:
                late.pop(i)[1]()
            else:
                i += 1

    def flush_late():
        while late:
            late.pop(0)[1]()

    psSc = [psF[0], psF[1], psS[2]]
    psAcc = [psF[2], psF[3], psS[3]]
    SKC = 2

    def unit(lhsT, rhs, r_l, r_r, bias_ap, bias_buf, vlhsT, vbuf, po, first, last):
        u = ucnt[0]
        ucnt[0] += 1
        ps = psSc[u % 3]
        pt_ = ptc[u % 5]
        if bias_ap is not None:
            MM(ps[:, 0:512], lhsT, rhs, True, False, [r_l, r_r], [ps])
            MM(ps[:, 0:512], ident_b[:], bias_ap, False, True, [ident_b, bias_buf], [ps])
        else:
            MM(ps[:, 0:512], lhsT, rhs, True, True, [r_l, r_r], [ps])
        ACT(pt_[:], ps[:, 0:512], AF.Exp, [ps], [pt_])
        run_pend(SKC - 1)
        pend.append(lambda: MM(po[0:65, 0:512], vlhsT, pt_[:], first, last, [vbuf, pt_], [po]))

    def finalize(po, br, grt, tail):
        ob = obr[bcnt[0] % 5]
        bcnt[0] += 1

        def part1():
            CP("dve", ob[0:65, :], po[0:65, 0:512], [po], [ob])
            TS(ob[64:65, :], ob[64:65, :], 1e-30, None, ALU.max, None, [ob], [ob])
            P.op("dve", lambda e, ob=ob: e.reciprocal(out=ob[64:65, :], in_=ob[64:65, :]), r=[ob], w=[ob])
            TT(ob[64:65, :], ob[64:65, :], grt[64:65, br, :, :].rearrange("p h q -> p (h q)"), ALU.mult, [ob, grt], [ob])

        def part2():
            pbc = psF[4]
            MM(pbc[0:64, 0:512], ones_f[64:65, 0:64], ob[64:65, :], True, True, [ones_f, ob], [pbc])
            if br == 0:
                TT(oacc[0:64, :], ob[0:64, :], pbc[0:64, 0:512], ALU.mult, [ob, pbc], [oacc])
            else:
                TT(otmp[0:64, :], ob[0:64, :], pbc[0:64, 0:512], ALU.mult, [ob, pbc], [otmp])
                if tail is None:
                    TT(oacc[0:64, :], oacc[0:64, :], otmp[0:64, :], ALU.add, [oacc, otmp], [oacc])
                else:
                    tail()
        late.append([SKC + 1, part1])
        late.append([SKC + 22, part2])

    OBv = OB.rearrange("h d t -> d h t")

    def front(g, qi):
        c0 = 8 + 4 * g
        o0 = 128 * qi
        qb = qi % 2
        sa, sv, cmt, cmT, grt = sa_t[qb], sv_t[qb], cmt_t[qb], cmT_t[qb], GRt[qb]
        DMA("sp", sa[:], IN["sel_add"][:, qi, :], [], [sa])
        DMA("sp", sv[:], IN["sel_valid"][:, qi, :], [], [sv])
        DMA("sp", cmt[:], IN["cm_tm"][qi, :, :], [], [cmt])
        DMA("sp", cmT[:], IN["cm_T"][qi, :, :, :], [], [cmT])
        DMA("sp", grt[64:65, :, :, :], NG.rearrange("(b h) t -> b h t", b=3)[:, 4 * g:4 * g + 4, o0:o0 + 128].rearrange("(o b) h t -> o b h t", o=1), [], [grt])
        qa, qbB, qw = QNsA[qb], QNsB[qb], QNw[qb]
        qsrc = QG[0:64, :, o0:o0 + 128]
        CP("pool", qa[0:64, :, :], qsrc, [QG], [qa])
        CP("pool", qbB[0:64, :, :], qsrc, [QG], [qbB])
        CP("pool", qw[0:64, :, :], qsrc, [QG], [qw])
        yield
        for hh in range(4):
            pc = psF[5]
            pe = pe_t[hh % 2]
            MM(pc[:, 0:512], QG[0:64, hh, o0:o0 + 128], KC[0:64, :], True, False, [QG, KC], [pc])
            MM(pc[:, 0:512], ident_b[:], cmt[:], False, True, [ident_b, cmt], [pc])
            ACT(pe[:], pc[:, 0:512], AF.Exp, [pc], [pe, rs_t], accum_out=rs_t[:, hh:hh + 1])
            TS(rs_t[:, hh:hh + 1], rs_t[:, hh:hh + 1], 1e-30, None, ALU.max, None, [rs_t], [rs_t])
            P.op("dve", lambda e, hh=hh: e.reciprocal(out=rs_t[:, 4 + hh:5 + hh], in_=rs_t[:, hh:hh + 1]), r=[rs_t], w=[rs_t])
            if hh == 0:
                TS(pgx[:, 1:513], pe[:], rs_t[:, 4:5], None, ALU.mult, None, [pe, rs_t], [pgx])
            else:
                STT(pgx[:, 1:513], pe[:], rs_t[:, 4 + hh:5 + hh], pgx[:, 1:513], ALU.mult, ALU.add, [pe, rs_t, pgx], [pgx])
            yield
        V = [pgx[:, r:r + 509:4] for r in range(5)]
        TT(t1b[:], V[0], V[4], ALU.add, [pgx], [t1b])
        TT(t2b[:], V[1], V[2], ALU.add, [pgx], [t2b])
        TT(t2b[:], t2b[:], V[3], ALU.add, [pgx, t2b], [t2b])
        STT(impb[:], t2b[:], 2.0, t1b[:], ALU.mult, ALU.add, [t2b, t1b], [impb])
        TT(impb[:], impb[:], sa[:], ALU.add, [impb, sa], [impb])
        P.op("dve", lambda e: e.max(out=m16[:, 0:8], in_=impb[:]), r=[impb], w=[m16])
        P.op("dve", lambda e: e.match_replace(out=wkb[:], in_to_replace=m16[:, 0:8], in_values=impb[:], imm_value=-3e4), r=[m16, impb], w=[wkb])
        P.op("dve", lambda e: e.max(out=m16[:, 8:16], in_=wkb[:]), r=[wkb], w=[m16])
        STT(selb[:], impb[:], m16[:, 15:16], sv[:], ALU.is_ge, ALU.mult, [impb, m16, sv], [selb])
        TS(nmc[:, 64:128], selb[:, 0:64], 30000.0, -30000.0, ALU.mult, ALU.add, [selb], [nmc])
        TS(nmc[:, 128:192], selb[:, 64:128], 30000.0, -30000.0, ALU.mult, ALU.add, [selb], [nmc])
        yield
        yield
        yield
        pt5 = psF[5]
        TR(pt5[:, 0:128], nmc[:, 0:128], ident_f[:], [nmc, ident_f], [pt5])
        TR(pt5[:, 128:256], nmc[:, 64:192], ident_f[:], [nmc, ident_f], [pt5])
        c31b = bcast(c31t[64:128, c0:c0 + 4], 2, 128)
        TT(qa[64:128, :, :], bcast(pt5[64:128, 0:128], 1, 4), c31b, ALU.add, [pt5, c31t], [qa])
        TT(qbB[64:128, :, :], bcast(pt5[64:128, 128:256], 1, 4), c31b, ALU.add, [pt5, c31t], [qbB])
        yield

    for g in (range(2) if upto >= 3 else []):
        run_pend()
        flush_late()
        DMA("sp", srcT[0:64, :], KT[4, 64 * g:64 * g + 64, :], [], [srcT])
        for hh in range(4):
            DMA("sp", QG[0:64, hh, :], NQ[2 * g + hh // 2, 64 * (hh % 2):64 * (hh % 2) + 64, :], [], [QG])
        DMA("sp", KS[0:64, :], KT[5, 64 * g:64 * g + 64, :], [], [KS])
        DMA("sp", KS[64:128, :], IN["e_sel"][:, :], [], [KS])
        DMA("sp", KW[0:64, :], KT[6, 64 * g:64 * g + 64, :], [], [KW])
        DMA("sp", KW[64:65, :], IN["kinv"][:, :], [], [KW])
        DMA("sp", VS[:], VT[8 + g, :, :], [], [VS])
        DMA("sp", VW[:], VT[10 + g, :, :], [], [VW])
        DMA("sp", TBs[:], IN["tb_sel"][g, :, :, :, :], [], [TBs])
        DMA("sp", TBw[:], IN["tb_win"][g, :, :, :, :], [], [TBw])
        c0 = 8 + 4 * g
        for oi in range(2):
            TT(TBsb[:, oi, :, :], TBs[:, oi, :, :], bcast(c31t[:, c0:c0 + 4], 2, 128), ALU.subtract, [TBs, c31t], [TBsb])
        CP("dve", TBwb[:], TBw[:], [TBw], [TBwb])
        for i, kgsrc in enumerate((4, 7)):
            p0 = 64 * i
            if i == 1:
                DMA("sp", srcT[p0:p0 + 64, :], KT[kgsrc, 64 * g:64 * g + 64, :], [], [srcT])
            CP("dve", srcD[p0:p0 + 64, :, :], srcT[p0:p0 + 64, :].rearrange("p (n r) -> p r n", r=16), [srcT], [srcD])
            for half in range(2):
                ph = psF[half]
                for j in range(32):
                    MM(ph[:, 0:511], W1[p0:p0 + 64, j, half * 128:(half + 1) * 128], srcD[p0:p0 + 64, j % 16, (j // 16):(j // 16) + 511], j == 0, j == 31, [W1, srcD], [ph])
                ACT(hid[i][:, half, 0:511], ph[:, 0:511], AF.Gelu_apprx_tanh, [ph, hbias[i]], [hid[i]], bias=hbias[i][:, half:half + 1])
            if i == 0:
                pk = psF[2]
                for half in range(2):
                    MM(pk[0:64, 0:512], W2[0][:, half, :], hid[0][:, half, :], half == 0, half == 1, [W2[0], hid[0]], [pk])
                CP("act", KC[0:64, :], pk[0:64, 0:512], [pk], [KC])
            else:
                for ct in range(4):
                    pv_ = psF[2 + ct % 2]
                    for half in range(2):
                        MM(pv_[:, 0:64], hid[1][:, half, ct * 128:(ct + 1) * 128], W2[1][:, half, :], half == 0, half == 1, [W2[1], hid[1]], [pv_])
                    CP("dve", VCp[:, ct, 0:64], pv_[:, 0:64], [pv_], [VCp])
        for _ in front(g, 0):
            pass
        for qi in range(17):
            lt = 47 + qi
            o0 = 128 * qi
            qb = qi % 2
            cmT, grt = cmT_t[qb], GRt[qb]
            qa, qbB, qw = QNsA[qb], QNsB[qb], QNw[qb]
            nxt = front(g, qi + 1) if qi + 1 < 17 else None
            po = psAcc[(3 * qi) % 3]
            for ct in range(4):
                unit(KC[0:64, ct * 128:(ct + 1) * 128], QG[0:64, :, o0:o0 + 128], KC, QG,
                     bcast(cmT[:, ct, :], 1, 4), cmT, VCp[:, ct, :], VCp, po, ct == 0, ct == 3)
            finalize(po, 0, grt, None)
            po = psAcc[(3 * qi + 1) % 3]
            for kt in range(lt + 1):
                rhs_b = qa if kt < 32 else qbB
                near = kt >= lt - 1
                unit(KS[:, kt * 128:(kt + 1) * 128], rhs_b[:, :, :], KS, rhs_b,
                     TBsb[:, kt - (lt - 1), :, :] if near else None, TBsb, VS[:, kt * 65:(kt + 1) * 65], VS, po, kt == 0, kt == lt)
                if nxt is not None and kt >= 2 and kt % 3 == 0:
                    next(nxt, None)
            finalize(po, 1, grt, None)
            if nxt is not None:
                for _ in nxt:
                    pass
            po = psAcc[(3 * qi + 2) % 3]
            for oi in range(5):
                kt = lt - 4 + oi
                unit(KW[0:65, kt * 128:(kt + 1) * 128], qw[0:65, :, :], KW, qw,
                     TBwb[:, oi, :, :], TBwb, VW[:, kt * 65:(kt + 1) * 65], VW, po, oi == 0, oi == 4)

            def tail(qi=qi, g=g, o0=o0):
                ob_ = obb[qi % 2]
                TT(ob_[0:64, :], oacc[0:64, :], otmp[0:64, :], ALU.add, [oacc, otmp], [ob_])
                DMA("sp", OBv[:, 4 * g:4 * g + 4, o0:o0 + 128], ob_[0:64, :].rearrange("p (h q) -> p h q", h=4), [ob_], [])
            finalize(po, 2, grt, tail)
    run_pend()
    flush_late()
    P.barrier()
    A.off = base_mark
    if upto <= 3:
        return finish(nc, P, A)

    X1 = scr("X1", [NO, D], F32)
    TBLK = [(0, 128), (128, 512), (640, 512), (1152, 512), (1664, 512)]
    WA = A.alloc("WA", [8, D], BF16)
    WB = A.alloc("WB", [8, D], BF16)
    WO = A.alloc("WO", [8, D], BF16)
    wstg = [A.alloc("wstg%d" % i, [D], F32) for i in range(2)]
    OAt = A.alloc("OAt", [8, 512], BF16)
    OBt = A.alloc("OBt", [8, 512], BF16)
    GA = A.alloc("GA", [8, 512], BF16)
    GB = A.alloc("GB", [8, 512], BF16)
    mT = A.alloc("mT", [8, 512], BF16)
    tm1 = [A.alloc("tm1%d" % i, [512], F32) for i in range(2)]
    tm2 = [A.alloc("tm2%d" % i, [512], F32) for i in range(2)]
    xin = [A.alloc("xin%d" % i, [D], F32) for i in range(2)]
    x1t = [A.alloc("x1t%d" % i, [D], F32) for i in range(2)]
    wl = [0]

    def load_w(dst_ap, dst_buf, src_ap, nparts, ncols):
        ws = wstg[wl[0] % 2]
        wl[0] += 1
        DMA("sp", ws[0:nparts, 0:ncols], src_ap, [], [ws])
        CP(("dve", "pool")[wl[0] % 2], dst_ap, ws[0:nparts, 0:ncols], [ws], [dst_buf])

    DMA("sp", WA[:, 0:4, :], WAs.rearrange("(hp r) n -> r hp n", r=128), [], [WA])
    DMA("sp", WB[:, 0:4, :], WBs.rearrange("(hp r) n -> r hp n", r=128), [], [WB])
    DMA("sp", WO[:, :, :], WOs.rearrange("(f p) n -> p f n", p=128), [], [WO])
    OAv = OA.rearrange("h d t -> d h t")
    OBv2 = OB.rearrange("h d t -> d h t")
    GAv = GAB.rearrange("f p t -> p f t")
    xcnt = [0]
    for (o0, N) in TBLK:
        for two in range(2):
            DMA("sp", OAt[64 * two:64 * two + 64, 0:4, 0:N], OA.rearrange("(hp two) d t -> two d hp t", two=2)[two, :, :, o0:o0 + N], [], [OAt])
            DMA("sp", OBt[64 * two:64 * two + 64, 0:4, 0:N], OB.rearrange("(hp two) d t -> two d hp t", two=2)[two, :, :, o0:o0 + N], [], [OBt])
        DMA("sp", GA[:, :, 0:N], GAv[:, 0:8, o0:o0 + N], [], [GA])
        DMA("sp", GB[:, :, 0:N], GAv[:, 8:16, o0:o0 + N], [], [GB])
        for f in range(8):
            pA, pB = psF[f % 2], psF[2 + f % 2]
            for hp in range(4):
                MM(pA[:, 0:N], WA[:, hp, f * 128:(f + 1) * 128], OAt[:, hp, 0:N], hp == 0, hp == 3, [WA, OAt], [pA])
            for hp in range(4):
                MM(pB[:, 0:N], WB[:, hp, f * 128:(f + 1) * 128], OBt[:, hp, 0:N], hp == 0, hp == 3, [WB, OBt], [pB])
            t1_, t2_ = tm1[f % 2], tm2[f % 2]
            TT(t1_[:, 0:N], pA[:, 0:N], GA[:, f, 0:N], ALU.mult, [pA, GA], [t1_])
            TT(t2_[:, 0:N], pB[:, 0:N], GB[:, f, 0:N], ALU.mult, [pB, GB], [t2_])
            TT(mT[:, f, 0:N], t1_[:, 0:N], t2_[:, 0:N], ALU.add, [t1_, t2_], [mT])
        for s_ in range(N // 128):
            xi = xin[xcnt[0] % 2]
            xo = x1t[xcnt[0] % 2]
            xcnt[0] += 1
            r0 = O0 + o0 + s_ * 128
            DMA("sp", xi[:], xl[r0:r0 + 128, :], [], [xi])
            for half in range(2):
                ps = psF[4 + half]
                for f in range(8):
                    MM(ps[:, :], mT[:, f, s_ * 128:(s_ + 1) * 128], WO[:, f, half * 512:(half + 1) * 512], f == 0, f == 7, [mT, WO], [ps])
                TT(xo[:, half * 512:(half + 1) * 512], ps[:, :], xi[:, half * 512:(half + 1) * 512], ALU.add, [ps, xi], [xo])
            DMA("sp", X1[o0 + s_ * 128:o0 + (s_ + 1) * 128, :], xo[:], [xo], [])
    P.barrier()
    A.off = base_mark
    if upto <= 4:
        return finish(nc, P, A)

    WXQ = A.alloc("WXQ", [8, 512], BF16)
    WXO = A.alloc("WXO", [4, D], BF16)
    WXKV = A.alloc("WXKV", [8, D], BF16)
    XK = A.alloc("XK", [4, 256], BF16)
    XV = A.alloc("XV", [2, 4, 128], BF16)
    gXT = A.alloc("gXT", [8], F32)
    gFT = A.alloc("gFT", [8], F32)
    gMT = A.alloc("gMT", [8], F32)
    gO = A.alloc("gO", [D], F32)
    cw = A.alloc("cw", [NFC, 3], F32)
    cb = A.alloc("cb", [NFC], F32)
    carry = A.alloc("carry", [NFC, 2], F32)
    wstg = [A.alloc("wstgb%d" % i, [D], F32) for i in range(2)]
    x1sL = [A.alloc("x1s%d" % i, [4, D], F32) for i in range(2)]
    x1s = x1sL[0]
    junk2 = A.alloc("junk2", [D], BF16)
    xn2 = [A.alloc("xn2%d" % i, [D], BF16) for i in range(4)]
    stat2 = A.alloc("stat2", [12], F32)
    hT2L = [A.alloc("hT2%d" % i, [8, 512], BF16) for i in range(2)]
    hT2 = hT2L[0]
    hTf = A.alloc("hTf", [8, 512], BF16)
    xqT = A.alloc("xqT", [4, 512], BF16)
    ptd = [A.alloc("ptd%d" % i, [512], BF16) for i in range(3)]
    oxs = [A.alloc("oxs%d" % i, [512], F32) for i in range(2)]
    rsx = [A.alloc("rsx%d" % i, [512], F32) for i in range(2)]
    oxT = A.alloc("oxT", [4, 512], BF16)
    WGc = [A.alloc("WGc%d" % i, [8, 128], BF16) for i in range(3)]
    WUc = [A.alloc("WUc%d" % i, [8, 128], BF16) for i in range(3)]
    WDc = [A.alloc("WDc%d" % i, [512], BF16) for i in range(8)]
    aEt = [A.alloc("aE%d" % i, [516], F32) for i in range(2)]
    cvt = [A.alloc("cv%d" % i, [512], F32) for i in range(2)]
    glt = [A.alloc("gl%d" % i, [512], F32) for i in range(2)]
    gTl = [A.alloc("gT%d" % i, [512], BF16) for i in range(NFC)]
    outt = [A.alloc("outt%d" % i, [D], F32) for i in range(2)]

    DMA("sp", WXQ[:, :, :], WXQs.rearrange("(f p) n -> p f n", p=128), [], [WXQ])
    DMA("sp", WXKV[:, :, :], WXKVs.rearrange("(f p) n -> p f n", p=128), [], [WXKV])
    DMA("sp", WXO[:, :, :], WXOs.rearrange("(h p) n -> p h n", p=128), [], [WXO])
    for gt_, nm_ in ((gXT, "norm_xattn_g"), (gFT, "norm_ffn_g"), (gMT, "norm_mem_g")):
        DMA("sp", gt_[:], IN[nm_].rearrange("o (k p) -> p (o k)", p=128), [], [gt_], slow=True)
    DMA("sp", gO[:], IN["norm_final_g"][0:1, :].broadcast_to([128, D]), [], [gO])
    for tap in range(3):
        DMA("sp", cw[:, :, tap:tap + 1], IN["conv_w"][tap:tap + 1, :].rearrange("o (c p) -> p c o", p=128), [], [cw], slow=True)
    DMA("sp", cb[:], IN["conv_b"].rearrange("o (c p) -> p (o c)", p=128), [], [cb], slow=True)
    MEMSET("pool", carry[:], 0.0, [carry])
    for i in range(2):
        MEMSET("pool", aEt[i][:], 0.0, [aEt[i]])
    for s_ in range(2):
        DMA("sp", x1s[:, s_, :], IN["mem"][s_ * 128:(s_ + 1) * 128, :], [], [x1s])
    norm_block([(x1s, x1s[:, s_, :]) for s_ in range(2)], None, junk2, stat2, xn2, hT2, 256, gainT=gMT)
    for hd in range(4):
        ps = psF[hd % 2]
        for f in range(8):
            MM(ps[:, 0:256], WXKV[:, f, hd * 128:(hd + 1) * 128], hT2[:, f, 0:256], f == 0, f == 7, [WXKV, hT2], [ps])
        CP("act", XK[:, hd, :], ps[:, 0:256], [ps], [XK])
    for kt in range(2):
        ps = psF[2 + kt]
        for f in range(8):
            MM(ps[:, :], hT2[:, f, kt * 128:(kt + 1) * 128], WXKV[:, f, 512:1024], f == 0, f == 7, [WXKV, hT2], [ps])
        CP("dve", XV[:, kt, :, :].rearrange("p h d -> p (h d)"), ps[:, :], [ps], [XV])

    XSC = 128.0 ** -0.5
    pendx = []
    ucnt = [0]
    ocnt = [0]

    def d2_load(tb):
        o0, N = TBLK[tb]
        xs_ = x1sL[tb % 2]
        for s_ in range(N // 128):
            DMA("sp", xs_[:, s_, :], X1[o0 + s_ * 128:o0 + (s_ + 1) * 128, :], [], [xs_])

    def d2_norm1(tb):
        o0, N = TBLK[tb]
        xs_ = x1sL[tb % 2]
        norm_block([(xs_, xs_[:, s_, :]) for s_ in range(N // 128)], None, junk2, stat2, xn2, hT2L[tb % 2], N, gainT=gXT)

    d2_load(0)
    d2_norm1(0)
    for tb, (o0, N) in enumerate(TBLK):
        ns = N // 128
        if tb + 1 < len(TBLK):
            d2_load(tb + 1)
        x1s = x1sL[tb % 2]
        hT2 = hT2L[tb % 2]
        for hd in range(4):
            ps = psF[hd % 2]
            for f in range(8):
                MM(ps[:, 0:N], WXQ[:, f, hd * 128:(hd + 1) * 128], hT2[:, f, 0:N], f == 0, f == 7, [WXQ, hT2], [ps])
            CP("act", xqT[:, hd, 0:N], ps[:, 0:N], [ps], [xqT])
        finx = []
        for hd in range(4):
            po = psF[2 + hd % 2]
            psm = psF[4]
            for kt in range(2):
                u = ucnt[0]
                ucnt[0] += 1
                ps = psF[u % 2]
                pt_ = ptd[u % 3]
                MM(ps[:, 0:N], XK[:, hd, kt * 128:(kt + 1) * 128], xqT[:, hd, 0:N], True, True, [XK, xqT], [ps])
                ACT(pt_[:, 0:N], ps[:, 0:N], AF.Exp, [ps], [pt_], scale=XSC)
                while len(pendx) > 0:
                    pendx.pop(0)()

                def pvx(po=po, psm=psm, kt=kt, hd=hd, pt_=pt_, N=N):
                    MM(po[:, 0:N], XV[:, kt, hd, :], pt_[:, 0:N], kt == 0, kt == 1, [XV, pt_], [po])
                    MM(psm[0:1, 0:N], ones_b[:, 0:1], pt_[:, 0:N], kt == 0, kt == 1, [ones_b, pt_], [psm])
                pendx.append(pvx)
                if kt == 0:
                    while finx:
                        finx.pop(0)()
            while len(pendx) > 0:
                pendx.pop(0)()
            ox, rx = oxs[hd % 2], rsx[hd % 2]
            CP("act", ox[:, 0:N], po[:, 0:N], [po], [ox])
            ACT(rx[0:1, 0:N], psm[0:1, 0:N], AF.Ln, [psm], [rx])
            ACT(rx[0:1, 0:N], rx[0:1, 0:N], AF.Exp, [rx], [rx], scale=-1.0)

            def finh(hd=hd, ox=ox, rx=rx, N=N):
                pbc = psF[5]
                MM(pbc[:, 0:N], ones_f[0:1, 0:128], rx[0:1, 0:N], True, True, [ones_f, rx], [pbc])
                TT(oxT[:, hd, 0:N], ox[:, 0:N], pbc[:, 0:N], ALU.mult, [ox, pbc], [oxT])
            finx.append(finh)
        while finx:
            finx.pop(0)()
        for s_ in range(ns):
            for half in range(2):
                ps = psF[half]
                for hd in range(4):
                    MM(ps[:, :], oxT[:, hd, s_ * 128:(s_ + 1) * 128], WXO[:, hd, half * 512:(half + 1) * 512], hd == 0, hd == 3, [oxT, WXO], [ps])
                TT(x1s[:, s_, half * 512:(half + 1) * 512], ps[:, :], x1s[:, s_, half * 512:(half + 1) * 512], ALU.add, [ps, x1s], [x1s])
        norm_block([(x1s, x1s[:, s_, :]) for s_ in range(ns)], None, junk2, stat2, xn2, hTf, N, gainT=gFT)
        for fc in range(NFC):
            wg, wu = WGc[fc % 3], WUc[fc % 3]
            DMA("sp", wg[:], WGs[fc, :, :, :], [], [wg])
            DMA("sp", wu[:], WUs[fc, :, :, :], [], [wu])
            pG, pU = psF[fc % 2], psF[2 + fc % 2]
            for f in range(8):
                MM(pG[:, 0:N], wg[:, f, :], hTf[:, f, 0:N], f == 0, f == 7, [wg, hTf], [pG])
            for f in range(8):
                MM(pU[:, 0:N], wu[:, f, :], hTf[:, f, 0:N], f == 0, f == 7, [wu, hTf], [pU])
            aE, cv, gl = aEt[fc % 2], cvt[fc % 2], glt[fc % 2]
            CP("dve", aE[:, 0:2], carry[:, fc, :], [carry], [aE])
            CP("act", aE[:, 2:2 + N], pG[:, 0:N], [pG], [aE])
            if tb == 0:
                TS(carry[:, fc, :], aE[:, N:N + 2], flag[:, 0:1], None, ALU.mult, None, [aE, flag], [carry])
            else:
                CP("dve", carry[:, fc, :], aE[:, N:N + 2], [aE], [carry])
            TS(cv[:, 0:N], aE[:, 0:N], cw[:, fc, 0:1], None, ALU.mult, None, [aE, cw], [cv])
            STT(cv[:, 0:N], aE[:, 1:N + 1], cw[:, fc, 1:2], cv[:, 0:N], ALU.mult, ALU.add, [aE, cw, cv], [cv])
            STT(cv[:, 0:N], aE[:, 2:N + 2], cw[:, fc, 2:3], cv[:, 0:N], ALU.mult, ALU.add, [aE, cw, cv], [cv])
            ACT(gl[:, 0:N], cv[:, 0:N], AF.Gelu_apprx_tanh, [cv, cb], [gl], bias=cb[:, fc:fc + 1])
            TT(gTl[fc][:, 0:N], gl[:, 0:N], pU[:, 0:N], ALU.mult, [gl, pU], [gTl[fc]])
        for half in range(2):
            for fc in range(NFC):
                wd = WDc[(half * NFC + fc) % 8]
                DMA("sp", wd[:], WDs[fc, :, half * 512:(half + 1) * 512], [], [wd])
                for s_ in range(ns):
                    MM(psF[s_][:, :], gTl[fc][:, s_ * 128:(s_ + 1) * 128], wd[:], fc == 0, fc == NFC - 1, [gTl[fc], wd], [psF[s_]])
            if half == 1 and tb + 1 < len(TBLK):
                d2_norm1(tb + 1)
            for s_ in range(ns):
                TT(x1s[:, s_, half * 512:(half + 1) * 512], psF[s_][:, :], x1s[:, s_, half * 512:(half + 1) * 512], ALU.add, [psF[s_], x1s], [x1s])
        if tb >= 1:
            for s_ in range(ns):
                ACT(junk2[:], x1s[:, s_, :], AF.Square, [x1s], [junk2, stat2], accum_out=stat2[:, s_:s_ + 1])
            ACT(stat2[:, 4:4 + ns], stat2[:, 0:ns], AF.Sqrt, [stat2], [stat2], scale=1.0 / D, bias=EPS_T[:, 0:1])
            P.op("dve", lambda e, ns=ns: e.reciprocal(out=stat2[:, 8:8 + ns], in_=stat2[:, 4:4 + ns]), r=[stat2], w=[stat2])
            for s_ in range(ns):
                ot = outt[ocnt[0] % 2]
                ocnt[0] += 1
                STT(ot[:], x1s[:, s_, :], stat2[:, 8 + s_:9 + s_], gO[:], ALU.mult, ALU.mult, [x1s, stat2, gO], [ot])
                r0 = o0 - 128 + s_ * 128
                DMA("sp", out[r0:r0 + 128, :], ot[:], [ot], [])
    return finish(nc, P, A)


def finish(nc, P, A):
    P.finish_waits("sp")
    with nc.Block() as block:
        P.run(block)
    print("build: ninst=%d nwaits=%d arena_peak=%d" % (P.ninst, P.nwaits, A.peak))
    return nc


def make_in_maps(inputs):
    x = np.asarray(inputs["x"], np.float32)
    rel_bias = np.asarray(inputs["rel_bias"], np.float32)
    maps = []
    tabs = [host_tables(rel_bias, c) for c in range(4)]
    for core in range(8):
        b, c = core // 4, core % 4
        pad = 2048 * (3 - c)
        xl = np.zeros((L, D), np.float32)
        xl[pad:] = x[b, :2048 * (c + 1)]
        m = {"xl": xl}
        for n_, s_ in WEIGHT_SPECS:
            a = np.asarray(inputs[n_], np.float32)
            if n_ == "mem":
                a = a[b]
            m[n_] = np.ascontiguousarray(a.reshape(s_))
        for n_, s_, d_ in TABLE_SPECS:
            m[n_] = np.ascontiguousarray(tabs[c][n_])
        maps.append(m)
    return maps


_NC_CACHE = {}


def kernel(**inputs):
    if "nc" not in _NC_CACHE:
        _NC_CACHE["nc"] = build()
    nc = _NC_CACHE["nc"]
    maps = make_in_maps(inputs)
    res = run_bass_kernel_spmd(nc, maps, core_ids=list(range(8)))
    out = np.zeros((2, 8192, D), np.float32)
    for core in range(8):
        b, c = core // 4, core % 4
        out[b, 2048 * c:2048 * (c + 1)] = res.results[core]["out"]
    return out
```

```python
import math
import numpy as np
import ml_dtypes
import concourse.bass as bass
import concourse.mybir as mybir
from concourse.bass_utils import run_bass_kernel_spmd

F32 = mybir.dt.float32
BF16 = mybir.dt.bfloat16
AF = mybir.ActivationFunctionType
ALU = mybir.AluOpType
AX = mybir.AxisListType

ENGS = ("pe", "act", "dve", "pool", "sp")
NEG = -30000.0
L = 8192
NO = 2176
O0 = 6016
D = 1024
DFF = 2816
NFC = 22


class Buf:
    __slots__ = ("t", "lw", "rd", "name", "psum")

    def __init__(self, t, name="", psum=False):
        self.psum = psum
        self.t = t
        self.lw = None
        self.rd = {}
        self.name = name

    def __getitem__(self, idx):
        return self.t[idx]


class Prog:
    NDMA = 8

    def __init__(self, nc):
        self.nc = nc
        self.ops = {e: [] for e in ENGS}
        self.cnt = {e: 0 for e in ENGS}
        self.sem = {e: nc.alloc_semaphore(name="s_" + e) for e in ENGS}
        self.known = {e: {} for e in ENGS}
        self.dsem, self.dcnt, self.dnext = {}, {}, {}
        for e in ("sp", "act", "pool"):
            self.dsem[e] = [nc.alloc_semaphore(name="d_%s%d" % (e, i)) for i in range(self.NDMA)]
            self.dcnt[e] = [0] * self.NDMA
            self.dnext[e] = 0
        self.semobj = {}
        for e in ENGS:
            self.semobj[("e", e)] = self.sem[e]
        for e in self.dsem:
            for i, s in enumerate(self.dsem[e]):
                self.semobj[("d", e, i)] = s
        self.ninst = 0
        self.nwaits = 0

    def _need(self, eng, key, val, waits):
        if self.known[eng].get(key, 0) >= val:
            return
        if waits.get(key, 0) < val:
            waits[key] = val

    def _emit_waits(self, eng, waits):
        for key, val in waits.items():
            self.known[eng][key] = val
            self.ops[eng].append(("w", self.semobj[key], val))
            self.nwaits += 1

    def _collect(self, eng, r, w, skip_same, waits):
        me = ("e", eng)
        for b in r:
            if b.lw is not None and not (skip_same and b.lw[0] == me):
                self._need(eng, b.lw[0], b.lw[1], waits)
            if b.psum:
                for k, v in b.rd.items():
                    if k != me:
                        self._need(eng, k, v, waits)
        for b in w:
            if b.lw is not None and not (skip_same and b.lw[0] == me):
                self._need(eng, b.lw[0], b.lw[1], waits)
            for k, v in b.rd.items():
                if not (skip_same and k == me):
                    self._need(eng, k, v, waits)

    def _record(self, ev, r, w):
        for b in r:
            if b.rd.get(ev[0], 0) < ev[1]:
                b.rd[ev[0]] = ev[1]
        for b in w:
            b.lw = ev
            b.rd = {}

    def op(self, eng, fn, r=(), w=(), skip_same=False):
        waits = {}
        self._collect(eng, r, w, skip_same, waits)
        self._emit_waits(eng, waits)
        self.cnt[eng] += 1
        self.ops[eng].append(("i", fn, self.sem[eng], 1))
        ev = (("e", eng), self.cnt[eng])
        self._record(ev, r, w)
        self.ninst += 1
        return ev

    def dma(self, eng, fn, r=(), w=()):
        i = self.dnext[eng]
        self.dnext[eng] = (i + 1) % self.NDMA
        key = ("d", eng, i)
        waits = {}
        if self.dcnt[eng][i] > 0:
            self._need(eng, key, self.dcnt[eng][i], waits)
        for b in r:
            if b.lw is not None:
                self._need(eng, b.lw[0], b.lw[1], waits)
        for b in w:
            if b.lw is not None:
                self._need(eng, b.lw[0], b.lw[1], waits)
            for k, v in b.rd.items():
                self._need(eng, k, v, waits)
        self._emit_waits(eng, waits)
        self.dcnt[eng][i] += 16
        self.ops[eng].append(("i", fn, self.dsem[eng][i], 16))
        ev = (key, self.dcnt[eng][i])
        self._record(ev, r, w)
        self.ninst += 1
        return ev

    def _all_events(self):
        evs = []
        for e in ENGS:
            if self.cnt[e] > 0:
                evs.append((("e", e), self.cnt[e]))
        for e in self.dsem:
            for i in range(self.NDMA):
                if self.dcnt[e][i] > 0:
                    evs.append((("d", e, i), self.dcnt[e][i]))
        return evs

    def barrier(self):
        evs = self._all_events()
        for e in ENGS:
            waits = {}
            for k, v in evs:
                if k != ("e", e):
                    self._need(e, k, v, waits)
            self._emit_waits(e, waits)

    def finish_waits(self, eng="sp"):
        waits = {}
        for k, v in self._all_events():
            if k != ("e", eng):
                self._need(eng, k, v, waits)
        self._emit_waits(eng, waits)

    def run(self, block):
        def play(engobj, lst):
            for it in lst:
                if it[0] == "w":
                    engobj.wait_ge(it[1], it[2])
                else:
                    it[1](engobj).then_inc(it[2], it[3])

        @block.tensor
        def _(e):
            play(e, self.ops["pe"])

        @block.scalar
        def _(e):
            play(e, self.ops["act"])

        @block.vector
        def _(e):
            play(e, self.ops["dve"])

        @block.gpsimd
        def _(e):
            play(e, self.ops["pool"])

        @block.sync
        def _(e):
            play(e, self.ops["sp"])


class Arena:
    def __init__(self, nc, nbytes):
        self.t = nc.alloc_sbuf_tensor("arena", [128, nbytes // 2], BF16)
        self.cap = nbytes
        self.off = 0
        self.peak = 0

    def alloc(self, name, shape, dtype):
        isz = 4 if dtype == F32 else 2
        n = 1
        for s in shape:
            n *= s
        nb = (n * isz + 63) // 64 * 64
        assert self.off + nb <= self.cap, ("arena overflow", name, self.off, nb, self.cap)
        ap = self.t[:, self.off // 2:(self.off + n * isz) // 2]
        if dtype == F32:
            ap = ap.bitcast(F32)
        if len(shape) == 2:
            ap = ap.rearrange("p (a b) -> p a b", a=shape[0])
        elif len(shape) == 3:
            ap = ap.rearrange("p (a b c) -> p a b c", a=shape[0], b=shape[1])
        elif len(shape) == 4:
            ap = ap.rearrange("p (a b c d) -> p a b c d", a=shape[0], b=shape[1], c=shape[2])
        self.off += nb
        self.peak = max(self.peak, self.off)
        return Buf(ap, name)


def bcast(ap, pos, n):
    lst = [list(x) for x in ap.ap]
    lst.insert(pos, [0, n])
    return bass.AP(ap.tensor, ap.offset, lst)


def t5_bucket_np(d):
    n = np.maximum(d, 0)
    nf = np.maximum(n, 1).astype(np.float32)
    large = 16 + (np.log(nf / np.float32(16)) / np.float32(math.log(128 / 16)) * np.float32(16)).astype(np.int32)
    large = np.minimum(large, 31)
    return np.where(n < 16, n, large)


def host_tables(rel_bias, c):
    pad = 2048 * (3 - c)
    T = {}
    k = np.arange(128)[:, None]
    j = np.arange(512)[None, :]
    tb = np.empty((8, 128, 5, 512), np.float32)
    for oi in range(5):
        d = j - k - (oi - 1) * 128
        bk = t5_bucket_np(d)
        for h in range(8):
            tb[h, :, oi, :] = np.where(d >= 0, rel_bias[bk, h], np.float32(NEG))
    T["tb_moba"] = tb
    T["c31"] = np.broadcast_to(rel_bias[31][None, :], (128, 16)).astype(np.float32).copy()
    am = np.zeros((128, 17, 32), np.float32)
    pv = np.zeros((128, 17, 32), np.float32)
    ow = np.zeros((128, 17, 32), np.float32)
    padb = pad // 256
    for qi in range(17):
        cur = (47 + qi) // 2
        n = np.arange(32)
        val = (n >= padb) & (n < cur)
        am[:, qi, :] = np.where(val, 0.0, NEG)[None, :]
        pv[:, qi, :] = val.astype(np.float32)[None, :]
        ow[:, qi, :] = (n == cur).astype(np.float32)[None, :]
    T["mg_add"], T["mg_pv"], T["mg_own"] = am, pv, ow
    kk = np.arange(L)
    T["e_moba"] = (kk[None, :] // 256 == np.arange(32)[:, None]).astype(ml_dtypes.bfloat16)
    T["e_sel"] = ((kk[None, :] // 64) % 64 == np.arange(64)[:, None]).astype(ml_dtypes.bfloat16)
    T["kinv"] = (kk < pad).astype(ml_dtypes.bfloat16)[None, :]
    j = np.arange(128)[None, :]
    ts = np.empty((2, 128, 2, 4, 128), np.float32)
    tw = np.empty((2, 128, 5, 4, 128), np.float32)
    for g in range(2):
        for hh in range(4):
            h = 8 + 4 * g + hh
            for oi, off in enumerate((-128, 0)):
                d = j - k - off
                ts[g, :, oi, hh, :] = np.where(d >= 0, rel_bias[t5_bucket_np(d), h], np.float32(NEG))
            for oi, off in enumerate((-512, -384, -256, -128, 0)):
                d = j - k - off
                tw[g, :, oi, hh, :] = np.where((d >= 0) & (d < 512), rel_bias[t5_bucket_np(d), h], np.float32(NEG))
    T["tb_sel"], T["tb_win"] = ts, tw
    sa = np.zeros((128, 17, 128), np.float32)
    sv = np.zeros((128, 17, 128), np.float32)
    jl = np.arange(128)[None, :]
    for qi in range(17):
        tg = 128 * (47 + qi) + np.arange(128)[:, None] - pad
        curg = tg // 64
        jg = jl - pad // 64
        valid = (jg >= 0) & (jg <= curg) & (tg >= 0)
        forced = valid & ((jg == 0) | (jg == curg) | (jg == curg - 1))
        sa[:, qi, :] = np.where(forced, 1e4, np.where(valid, 0.0, -1e4))
        sv[:, qi, :] = valid.astype(np.float32)
    T["sel_add"], T["sel_valid"] = sa, sv
    cm = np.zeros((17, 128, 512), np.float32)
    n = np.arange(512)[None, :]
    for qi in range(17):
        tg = 128 * (47 + qi) + np.arange(128)[:, None] - pad
        ng = n - pad // 16
        vis = (ng >= 0) & (n <= 510) & (16 * ng + 31 <= tg)
        cm[qi] = np.where(vis, 0.0, NEG)
    T["cm_tm"] = cm.astype(ml_dtypes.bfloat16)
    T["cm_T"] = np.ascontiguousarray(cm.reshape(17, 128, 4, 128).transpose(0, 3, 2, 1)).astype(ml_dtypes.bfloat16)
    T["ident"] = np.eye(128, dtype=np.float32)
    T["flag"] = np.full((128, 1), 0.0 if c == 0 else 1.0, np.float32)
    return T


TABLE_SPECS = [
    ("tb_moba", [8, 128, 5, 512], F32), ("c31", [128, 16], F32),
    ("mg_add", [128, 17, 32], F32), ("mg_pv", [128, 17, 32], F32), ("mg_own", [128, 17, 32], F32),
    ("e_moba", [32, L], BF16), ("e_sel", [64, L], BF16), ("kinv", [1, L], BF16),
    ("tb_sel", [2, 128, 2, 4, 128], F32), ("tb_win", [2, 128, 5, 4, 128], F32),
    ("sel_add", [128, 17, 128], F32), ("sel_valid", [128, 17, 128], F32),
    ("cm_tm", [17, 128, 512], BF16), ("cm_T", [17, 128, 4, 128], BF16),
    ("ident", [128, 128], F32), ("flag", [128, 1], F32),
]

WEIGHT_SPECS = [
    ("mem", [256, D]), ("norm_mix_g", [1, D]), ("w_in", [D, 4888]),
    ("cmp_pos_k", [32, 64]), ("cmp_w1_k", [2048, 256]), ("cmp_w2_k", [256, 64]),
    ("cmp_pos_v", [32, 64]), ("cmp_w1_v", [2048, 256]), ("cmp_w2_v", [256, 64]),
    ("w_branch_a", [512, D]), ("w_branch_b", [512, D]), ("w_out", [D, D]),
    ("norm_xattn_g", [1, D]), ("norm_mem_g", [1, D]), ("w_xq", [D, 512]), ("w_xkv", [D, D]),
    ("w_xo", [512, D]), ("norm_ffn_g", [1, D]), ("w_gate", [D, DFF]), ("w_up", [D, DFF]),
    ("conv_w", [3, DFF]), ("conv_b", [1, DFF]), ("w_down", [DFF, D]), ("norm_final_g", [1, D]),
]

C_MQ, C_MK, C_MV, C_NQ = 0, 512, 1024, 1536
C_NKC, C_NVC, C_NKS, C_NVS, C_NKW, C_NVW = 2048, 2176, 2304, 2432, 2560, 2688
C_NG, C_GA, C_GB = 2816, 2840, 3864
KG_COLS = [C_MK, C_MK + 128, C_MK + 256, C_MK + 384, C_NKC, C_NKS, C_NKW, C_NVC]


def build(upto=99, debug=False, stop=None):
    nc = bass.Bass("TRN2", target_bir_lowering=False)
    P = Prog(nc)
    A = Arena(nc, 206 * 1024)
    IN = {}

    def din(name, shape, dt=F32):
        IN[name] = nc.dram_tensor(name, list(shape), dt, kind="ExternalInput").ap()

    din("xl", [L, D])
    for n_, s_ in WEIGHT_SPECS:
        din(n_, s_)
    for n_, s_, d_ in TABLE_SPECS:
        din(n_, s_, d_)
    out = nc.dram_tensor("out", [2048, D], F32, kind="ExternalOutput").ap()
    skind = "ExternalOutput" if debug else "Internal"

    def scr(name, shape, dt):
        return nc.dram_tensor(name, list(shape), dt, kind=skind).ap()

    KT = scr("KT", [8, 128, L], BF16)
    VT = scr("VT", [12, 128, 64 * 65], BF16)
    QS = scr("QS", [4, 128, NO], BF16)
    QF = scr("QF", [4, 128, NO], F32)
    NQ = scr("NQ", [4, 128, NO], BF16)
    GAB = scr("GAB", [16, 128, NO], BF16)
    NG = scr("NG", [24, NO], F32)
    OA = scr("OA", [8, 64, NO], BF16)
    OB = scr("OB", [8, 64, NO], BF16)
    WGs = scr("WGs", [NFC, 128, 8, 128], BF16)
    WUs = scr("WUs", [NFC, 128, 8, 128], BF16)
    WDs = scr("WDs", [NFC, 128, D], BF16)
    WAs = scr("WAs", [512, D], BF16)
    WBs = scr("WBs", [512, D], BF16)
    WOs = scr("WOs", [D, D], BF16)
    WXQs = scr("WXQs", [D, 512], BF16)
    WXKVs = scr("WXKVs", [D, D], BF16)
    WXOs = scr("WXOs", [512, D], BF16)

    psF = [Buf(nc.alloc_psum_tensor("psF%d" % i, [128, 512], F32), "psF%d" % i, psum=True) for i in range(6)]
    psT = [Buf(nc.alloc_psum_tensor("psT%d" % i, [128, 1024], BF16), "psT%d" % i, psum=True) for i in range(2)]

    def MM(out_, lhsT, rhs, start, stop, r, w):
        P.op("pe", lambda e: e.matmul(out_, lhsT=lhsT, rhs=rhs, start=start, stop=stop), r=r, w=w, skip_same=True)

    def TR(out_, in_, ident, r, w):
        P.op("pe", lambda e: e.transpose(out=out_, in_=in_, identity=ident), r=r, w=w, skip_same=True)

    def ACT(out_, in_, func, r, w, **kw):
        P.op("act", lambda e: e.activation(out=out_, in_=in_, func=func, **kw), r=r, w=w)

    def CP(eng, out_, in_, r, w):
        if eng == "act":
            P.op("act", lambda e: e.activation(out=out_, in_=in_, func=AF.Copy), r=r, w=w)
        else:
            P.op(eng, lambda e: e.tensor_copy(out=out_, in_=in_), r=r, w=w)

    def TT(out_, in0, in1, op, r, w, eng="dve"):
        P.op(eng, lambda e: e.tensor_tensor(out=out_, in0=in0, in1=in1, op=op), r=r, w=w)

    def TS(out_, in0, s1, s2, op0, op1, r, w, eng="dve"):
        if op1 is None:
            P.op(eng, lambda e: e.tensor_scalar(out=out_, in0=in0, scalar1=s1, scalar2=None, op0=op0), r=r, w=w)
        else:
            P.op(eng, lambda e: e.tensor_scalar(out=out_, in0=in0, scalar1=s1, scalar2=s2, op0=op0, op1=op1), r=r, w=w)

    def STT(out_, in0, scalar, in1, op0, op1, r, w, eng="dve"):
        P.op(eng, lambda e: e.scalar_tensor_tensor(out=out_, in0=in0, scalar=scalar, in1=in1, op0=op0, op1=op1), r=r, w=w)

    def MEMSET(eng, ap, val, w):
        P.op(eng, lambda e: e.memset(ap, val), w=w)

    def DMA(q, out_, in_, r, w, slow=False):
        if slow:
            P.dma(q, lambda e: e.dma_start(out=out_, in_=in_, allow_slow_non_contiguous=True), r=r, w=w)
        else:
            P.dma(q, lambda e: e.dma_start(out=out_, in_=in_), r=r, w=w)

    ident_f = A.alloc("ident_f", [128], F32)
    ident_b = A.alloc("ident_b", [128], BF16)
    ones_f = A.alloc("ones_f", [128], F32)
    ones_b = A.alloc("ones_b", [128], BF16)
    c31t = A.alloc("c31t", [16], F32)
    kmean = A.alloc("kmean", [4, 32], F32)
    flag = A.alloc("flag", [1], F32)
    DMA("sp", ident_f[:], IN["ident"][:, :], [], [ident_f])
    CP("dve", ident_b[:], ident_f[:], [ident_f], [ident_b])
    MEMSET("pool", ones_f[:], 1.0, [ones_f])
    MEMSET("pool", ones_b[:], 1.0, [ones_b])
    DMA("sp", c31t[:], IN["c31"][:, :], [], [c31t])
    DMA("sp", flag[:], IN["flag"][:, :], [], [flag])
    base_mark = A.off

    def norm_block(xs_list, gB, junk, stat, xn_list, hT_buf, ncols, gainT=None):
        for _ in norm_gen(xs_list, gB, junk, stat, xn_list, hT_buf, ncols, gainT):
            pass

    def norm_gen(xs_list, gB, junk, stat, xn_list, hT_buf, ncols, gainT=None):
        ns = len(xs_list)
        for s, (xb, xap) in enumerate(xs_list):
            ACT(junk[:], xap, AF.Square, [xb], [junk, stat], accum_out=stat[:, s:s + 1])
        ACT(stat[:, 4:4 + ns], stat[:, 0:ns], AF.Sqrt, [stat], [stat], scale=1.0 / D, bias=EPS_T[:, 0:1])
        P.op("dve", lambda e: e.reciprocal(out=stat[:, 8:8 + ns], in_=stat[:, 4:4 + ns]), r=[stat], w=[stat])
        yield
        nx = len(xn_list)
        for s, (xb, xap) in enumerate(xs_list):
            xn = xn_list[s % nx]
            if gainT is None:
                STT(xn[:], xap, stat[:, 8 + s:9 + s], gB[:], ALU.mult, ALU.mult, [xb, stat, gB], [xn])
            else:
                TS(xn[:], xap, stat[:, 8 + s:9 + s], None, ALU.mult, None, [xb, stat], [xn])
        for s, (xb, xap) in enumerate(xs_list):
            if s > 0:
                yield
            xn = xn_list[s % nx]
            pt = psT[s % 2]
            for k in range(8):
                TR(pt[:, k * 128:(k + 1) * 128], xn[:, k * 128:(k + 1) * 128], ident_b[:], [xn, ident_b], [pt])
            src = pt[:, :].rearrange("p (k t) -> p k t", k=8)
            if gainT is None:
                CP("act" if s % 2 == 0 else "dve", hT_buf[:, :, s * 128:(s + 1) * 128], src, [pt], [hT_buf])
            else:
                TT(hT_buf[:, :, s * 128:(s + 1) * 128], src, bcast(gainT[:, 0:8], 2, 128), ALU.mult, [pt, gainT], [hT_buf])

    EPS_T = A.alloc("eps_t", [1], F32)
    MEMSET("pool", EPS_T[:], 1e-6, [EPS_T])
    EPS30 = A.alloc("eps30", [1], F32)
    MEMSET("pool", EPS30[:], 1e-30, [EPS30])
    base_mark = A.off

    Wb = A.alloc("Wb", [8, 4888], BF16)
    WbQ = Buf(Wb.t, "WbQ")
    wst = [A.alloc("wst%d" % i, [1036], F32) for i in range(6)]
    gB = A.alloc("gB", [D], F32)
    xst = [A.alloc("xst%d" % i, [D], F32) for i in range(8)]
    junk = A.alloc("junk", [D], BF16)
    xn_l = [A.alloc("xn%d" % i, [D], BF16) for i in range(4)]
    hT = [A.alloc("hT%d" % i, [8, 512], BF16) for i in range(2)]
    stK = [A.alloc("stK%d" % i, [8, 512], BF16) for i in range(2)]
    stV = [A.alloc("stV%d" % i, [12, 4, 65], BF16) for i in range(2)]
    stQ = [A.alloc("stQ%d" % i, [512], BF16) for i in range(4)]
    stQF = [A.alloc("stQF%d" % i, [512], F32) for i in range(2)]
    stat = A.alloc("stat", [12], F32)

    DMA("sp", gB[:], IN["norm_mix_g"][0:1, :].broadcast_to([128, D]), [], [gB])
    for i in range(2):
        MEMSET("pool", stV[i][:, :, :, 64:65], 1.0, [stV[i]])
    wi = IN["w_in"]
    wchunks = [(k, c0, c1) for (c0, c1) in ((512, 1536), (2048, 2816)) for k in range(8)]
    wchunks += [(k, c0, c1) for (c0, c1) in ((0, 512), (1536, 2048), (2816, 3852), (3852, 4888)) for k in range(8)]
    wcount = [0]
    wpending = []

    def wcast():
        while wpending:
            i, k, c0, c1 = wpending.pop(0)
            ws = wst[i % 6]
            CP(("dve", "act", "pool")[i % 3], Wb[:, k, c0:c1], ws[:, 0:c1 - c0], [ws], [Wb if i < 16 else WbQ])

    def wload(n, cast_now=False):
        for _ in range(n):
            if wcount[0] >= len(wchunks):
                return
            k, c0, c1 = wchunks[wcount[0]]
            ws = wst[wcount[0] % 6]
            DMA("sp", ws[:, 0:c1 - c0], wi[k * 128:(k + 1) * 128, c0:c1], [], [ws])
            wpending.append((wcount[0], k, c0, c1))
            wcount[0] += 1
            if cast_now:
                wcast()

    wload(16, cast_now=True)

    if stop == 'a':
        return finish(nc, P, A)
    xl = IN["xl"]

    def load_x(blk):
        for s in range(4):
            ti = blk * 4 + s
            xs = xst[ti % 8]
            DMA("sp", xs[:], xl[ti * 128:(ti + 1) * 128, :], [], [xs])

    load_x(0)
    load_x(1)
    qcnt = [0]
    KTv = KT.rearrange("g p t -> p g t")
    VTv = VT.rearrange("h p f -> p h f")
    statA = [stat, A.alloc("statb", [12], F32)]

    def ngen(blk):
        return norm_gen([(xst[(blk * 4 + s) % 8], xst[(blk * 4 + s) % 8][:]) for s in range(4)],
                        gB, junk, statA[blk % 2], xn_l, hT[blk % 2], 512)

    for _ in ngen(0):
        pass
    for blk in range(16):
        if blk + 2 < 16:
            load_x(blk + 2)
        elif blk == 0:
            pass
        hb = hT[blk % 2]
        nxt_norm = ngen(blk + 1) if blk + 1 < 16 else None
        wcast()
        wload(3)
        if stop == 'b':
            continue
        sk = stK[blk % 2]
        for kg in range(8):
            ps = psF[kg % 2]
            c0 = KG_COLS[kg]
            for k in range(8):
                MM(ps[:, :], Wb[:, k, c0:c0 + 128], hb[:, k, :], k == 0, k == 7, [Wb, hb], [ps])
            CP("act", sk[:, kg, :], ps[:, :], [ps], [sk])
            if nxt_norm is not None and kg in (1, 3, 5, 7):
                next(nxt_norm, None)
            if kg < 4:
                P.op("dve", lambda e, ps=ps, kg=kg, blk=blk: e.tensor_reduce(
                    out=kmean[:, kg, 2 * blk:2 * blk + 2], in_=ps[:, :].rearrange("p (a b) -> p a b", a=2),
                    axis=AX.X, op=ALU.add), r=[ps], w=[kmean])
        DMA("pool", KTv[:, :, blk * 512:(blk + 1) * 512], sk[:], [sk], [])
        if stop == 'c':
            continue
        sv = stV[blk % 2]
        for s in range(4):
            ps = psF[2 + s % 2]
            for k in range(8):
                MM(ps[:, :], hb[:, k, s * 128:(s + 1) * 128], Wb[:, k, C_MV:C_MV + 512], k == 0, k == 7, [Wb, hb], [ps])
            CP("dve", sv[:, 0:8, s, 0:64], ps[:, :].rearrange("p (h e) -> p h e", h=8), [ps], [sv])
            ps2 = psF[4]
            for k in range(8):
                MM(ps2[:, 0:128], hb[:, k, s * 128:(s + 1) * 128], Wb[:, k, C_NVS:C_NVS + 128], k == 0, k == 7, [Wb, hb], [ps2])
            for k in range(8):
                MM(ps2[:, 128:256], hb[:, k, s * 128:(s + 1) * 128], Wb[:, k, C_NVW:C_NVW + 128], k == 0, k == 7, [Wb, hb], [ps2])
            CP("dve", sv[:, 8:12, s, 0:64], ps2[:, 0:256].rearrange("p (h e) -> p h e", h=4), [ps2], [sv])
        DMA("pool", VTv[:, :, blk * 260:(blk + 1) * 260], sv[:, :, :, :].rearrange("p h s e -> p h (s e)"), [sv], [])
        if nxt_norm is not None:
            for _ in nxt_norm:
                pass
        if stop == 'd':
            continue
        if blk >= 11:
            t0, n = (384, 128) if blk == 11 else (0, 512)
            o0 = blk * 512 + t0 - O0
            def qgroup(c0, m, kind, dst):
                i = qcnt[0]
                qcnt[0] += 1
                ps = psF[i % 2]
                for k in range(8):
                    MM(ps[0:m, 0:n], Wb[:, k, c0:c0 + m], hb[:, k, t0:t0 + n], k == 0, k == 7, [WbQ, hb], [ps])
                if kind == "q":
                    sq = stQ[i % 4]
                    ACT(sq[0:m, 0:n], ps[0:m, 0:n], AF.Copy, [ps], [sq], scale=0.125)
                    DMA("sp", dst[0], sq[0:m, 0:n], [sq], [])
                    if dst[1] is not None:
                        sf = stQF[i % 2]
                        CP("dve", sf[0:m, 0:n], ps[0:m, 0:n], [ps], [sf])
                        DMA("sp", dst[1], sf[0:m, 0:n], [sf], [])
                elif kind == "sig":
                    sq = stQ[i % 4]
                    ACT(sq[0:m, 0:n], ps[0:m, 0:n], AF.Sigmoid, [ps], [sq])
                    DMA("sp", dst[0], sq[0:m, 0:n], [sq], [])
                else:
                    sf = stQF[i % 2]
                    ACT(sf[0:m, 0:n], ps[0:m, 0:n], AF.Sigmoid, [ps], [sf])
                    DMA("sp", dst[0], sf[0:m, 0:n], [sf], [])
            for g in range(4):
                qgroup(C_MQ + 128 * g, 128, "q", (QS[g, :, o0:o0 + n], QF[g, :, o0:o0 + n]))
            for g in range(4):
                qgroup(C_NQ + 128 * g, 128, "q", (NQ[g, :, o0:o0 + n], None))
            for g in range(8):
                qgroup(C_GA + 128 * g, 128, "sig", (GAB[g, :, o0:o0 + n],))
            for g in range(8):
                qgroup(C_GB + 128 * g, 128, "sig", (GAB[8 + g, :, o0:o0 + n],))
            qgroup(C_NG, 24, "sigf", (NG[:, o0:o0 + n],))
    P.barrier()
    A.off = base_mark
    if upto <= 1:
        return finish(nc, P, A)

    psS = [psF[0], psF[1]] + [Buf(psT[i][:, :].bitcast(F32), "psS%d" % (2 + i), psum=True) for i in range(2)]
    SKEW = 3
    KE = [A.alloc("KE%d" % i, [L], BF16) for i in range(2)]
    VP = [A.alloc("VP%d" % i, [64 * 65], BF16) for i in range(2)]
    TBm = [A.alloc("TBm%d" % i, [5, 512], F32) for i in range(2)]
    mg_add = A.alloc("mg_add", [17, 32], F32)
    mg_pv = A.alloc("mg_pv", [17, 32], F32)
    mg_own = A.alloc("mg_own", [17, 32], F32)
    QN = [A.alloc("QN%d" % i, [512], BF16) for i in range(2)]
    QFt = [A.alloc("QFt%d" % i, [512], F32) for i in range(2)]
    nmx = [A.alloc("nmx%d" % i, [96], F32) for i in range(2)]
    gmb = [A.alloc("gmb%d" % i, [32], F32) for i in range(2)]
    alb = [A.alloc("alb%d" % i, [32], F32) for i in range(2)]
    m8 = [A.alloc("m8%d" % i, [8], F32) for i in range(2)]
    sst = [A.alloc("sst%d" % i, [512], F32) for i in range(4)]
    ptb = [A.alloc("ptb%d" % i, [512], BF16) for i in range(6)]
    osb = [A.alloc("osb%d" % i, [512], F32) for i in range(3)]
    oab = [A.alloc("oab%d" % i, [512], BF16) for i in range(3)]
    pcf = [A.alloc("pcf%d" % i, [DFF], F32) for i in range(2)]
    pcb = [A.alloc("pcb%d" % i, [DFF], BF16) for i in range(2)]

    for i in range(2):
        DMA("sp", KE[i][64:96, :], IN["e_moba"][:, :], [], [KE[i]])
        MEMSET("pool", nmx[i][:], 0.0, [nmx[i]])
    DMA("sp", mg_add[:], IN["mg_add"][:, :, :], [], [mg_add])
    DMA("sp", mg_pv[:], IN["mg_pv"][:, :, :], [], [mg_pv])
    DMA("sp", mg_own[:], IN["mg_own"][:, :, :], [], [mg_own])
    TS(kmean[:], kmean[:], 1.0 / 256.0, None, ALU.mult, None, [kmean], [kmean])

    def precast_steps():
        i = 0
        for W, Ws in ((IN["w_gate"], WGs), (IN["w_up"], WUs)):
            Wv = Ws.rearrange("c p f n -> p f c n")
            for f in range(8):
                pf, pb_ = pcf[i % 2], pcb[i % 2]
                DMA("pool", pf[:], W[f * 128:(f + 1) * 128, :], [], [pf])
                CP("pool", pb_[:], pf[:], [pf], [pb_])
                DMA("pool", Wv[:, f, :, :], pb_[:].rearrange("p (c n) -> p c n", c=NFC), [pb_], [])
                i += 1
                yield
        for c in range(NFC):
            pf, pb_ = pcf[i % 2], pcb[i % 2]
            DMA("pool", pf[:, 0:D], IN["w_down"][c * 128:(c + 1) * 128, :], [], [pf])
            CP("pool", pb_[:, 0:D], pf[:, 0:D], [pf], [pb_])
            DMA("pool", WDs[c, :, :], pb_[:, 0:D], [pb_], [])
            i += 1
            yield
        for (src, dst, R, C) in ((IN["w_branch_a"], WAs, 512, D), (IN["w_branch_b"], WBs, 512, D), (IN["w_out"], WOs, D, D),
                                 (IN["w_xq"], WXQs, D, 512), (IN["w_xkv"], WXKVs, D, D), (IN["w_xo"], WXOs, 512, D)):
            for r0 in range(0, R, 128):
                pf, pb_ = pcf[i % 2], pcb[i % 2]
                DMA("pool", pf[:, 0:C], src[r0:r0 + 128, :], [], [pf])
                CP("pool", pb_[:, 0:C], pf[:, 0:C], [pf], [pb_])
                DMA("pool", dst[r0:r0 + 128, :], pb_[:, 0:C], [pb_], [])
                i += 1
                yield

    pc_gen = precast_steps()

    def pc_step(n=1):
        for _ in range(n):
            try:
                next(pc_gen)
            except StopIteration:
                return

    def moba_load(h):
        g, pb = h // 2, 64 * (h % 2)
        hb = h % 2
        q_ = "sp" if h < 2 else "pool"
        DMA(q_, KE[hb][0:64, :], KT[g, pb:pb + 64, :], [], [KE[hb]])
        DMA(q_, VP[hb][:], VT[h, :, :], [], [VP[hb]])
        DMA(q_, TBm[hb][:], IN["tb_moba"][h, :, :, :], [], [TBm[hb]])

    CBS = [(0, 1), (1, 4), (5, 4), (9, 4), (13, 4)]
    ucnt = [0]
    pend = []

    late = []

    def run_pend(keep=0):
        while len(pend) > keep:
            pend.pop(0)()
        for it in late:
            it[0] -= 1
        i = 0
        while i < len(late):
            if late[i][0] <= 0:
                late.pop(i)[1]()
            else:
                i += 1

    def flush_late():
        while late:
            late.pop(0)[1]()

    def moba_prep(h, cbi, qi0, nt):
        g, pb = h // 2, 64 * (h % 2)
        N = 128 * nt
        o0 = 128 * qi0
        QNb, QFb = QN[cbi % 2], QFt[cbi % 2]
        DMA("sp", QNb[0:64, 0:N], QS[g, pb:pb + 64, o0:o0 + N], [], [QNb])
        DMA("sp", QFb[pb:pb + 64, 0:N], QF[g, pb:pb + 64, o0:o0 + N], [], [QFb])
        yield
        for j in range(nt):
            qi = qi0 + j
            x2 = (cbi * 4 + j) % 2
            pg = psF[4]
            MM(pg[:, 0:32], QFb[pb:pb + 64, j * 128:(j + 1) * 128], kmean[pb:pb + 64, g, :], True, True, [QFb, kmean], [pg])
            TT(gmb[x2][:], pg[:, 0:32], mg_add[:, qi, :], ALU.add, [pg, mg_add], [gmb[x2]])
            P.op("dve", lambda e, a=m8[x2], b=gmb[x2]: e.max(out=a[:], in_=b[:]), r=[gmb[x2]], w=[m8[x2]])
            STT(alb[x2][:], gmb[x2][:], m8[x2][:, 2:3], mg_pv[:, qi, :], ALU.is_ge, ALU.mult, [gmb[x2], m8[x2], mg_pv], [alb[x2]])
            TT(alb[x2][:], alb[x2][:], mg_own[:, qi, :], ALU.add, [alb[x2], mg_own], [alb[x2]])
            TS(nmx[x2][:, 64:96], alb[x2][:], 30000.0, -30000.0, ALU.mult, ALU.add, [alb[x2]], [nmx[x2]])
            yield
            yield
            pt5 = psF[5]
            TR(pt5[0:96, 0:128], nmx[x2][:, 0:96], ident_f[:], [nmx[x2], ident_f], [pt5])
            CP("dve", QNb[64:96, j * 128:(j + 1) * 128], pt5[64:96, 0:128], [pt5], [QNb])
            yield

    def moba_final(h, cbi, qi0, nt, po):
        N = 128 * nt
        o0 = 128 * qi0
        ob, oa = osb[cbi % 3], oab[cbi % 3]

        def part1():
            CP("dve", ob[0:65, 0:N], po[0:65, 0:N], [po], [ob])
            TS(ob[64:65, 0:N], ob[64:65, 0:N], 1e-30, None, ALU.max, None, [ob], [ob])
            P.op("dve", lambda e, ob=ob, N=N: e.reciprocal(out=ob[64:65, 0:N], in_=ob[64:65, 0:N]), r=[ob], w=[ob])

        def part2():
            pbc = psF[5]
            MM(pbc[0:64, 0:N], ones_f[64:65, 0:64], ob[64:65, 0:N], True, True, [ones_f, ob], [pbc])
            TT(oa[0:64, 0:N], ob[0:64, 0:N], pbc[0:64, 0:N], ALU.mult, [ob, pbc], [oa])
            DMA("sp", OA[h, :, o0:o0 + N], oa[0:64, 0:N], [oa], [])
        return part1, part2

    moba_list = [(h, qi0, nt) for h in range(8) for (qi0, nt) in CBS] if upto >= 2 else []
    if upto >= 2:
        moba_load(0)
        moba_load(1)
        prep = moba_prep(moba_list[0][0], 0, moba_list[0][1], moba_list[0][2])
        for _ in prep:
            pass
    for cbi, (h, qi0, nt) in enumerate(moba_list):
        hb = h % 2
        if qi0 == 0 and h >= 1 and h + 1 < 8:
            moba_load(h + 1)
        N = 128 * nt
        QNb = QN[cbi % 2]
        nxt = None
        if cbi + 1 < len(moba_list):
            h2, q2, n2 = moba_list[cbi + 1]
            nxt = moba_prep(h2, cbi + 1, q2, n2)
        pc_step(2)
        po = psF[2 + cbi % 2]
        nkt = 47 + qi0 + nt
        for kt in range(nkt):
            u = ucnt[0]
            ucnt[0] += 1
            ps = psS[u % 4]
            pt_ = ptb[u % 6]
            MM(ps[:, 0:N], KE[hb][0:96, kt * 128:(kt + 1) * 128], QNb[0:96, 0:N], True, True, [KE[hb], QNb], [ps])
            offi = kt - (47 + qi0) + 1
            if offi >= 0:
                ss = sst[u % 4]
                TT(ss[:, 0:N], ps[:, 0:N], TBm[hb][:, offi, 0:N], ALU.add, [ps, TBm[hb]], [ss])
                ACT(pt_[:, 0:N], ss[:, 0:N], AF.Exp, [ss], [pt_])
            else:
                ACT(pt_[:, 0:N], ps[:, 0:N], AF.Exp, [ps, c31t], [pt_], bias=c31t[:, h:h + 1])
            run_pend(SKEW - 1)
            pend.append(lambda po=po, hb=hb, kt=kt, pt_=pt_, N=N, nkt=nkt: MM(
                po[0:65, 0:N], VP[hb][:, kt * 65:(kt + 1) * 65], pt_[:, 0:N], kt == 0, kt == nkt - 1, [VP[hb], pt_], [po]))
            if nxt is not None and kt >= 2 and kt % 2 == 0:
                next(nxt, None)
        run_pend()
        if nxt is not None:
            for _ in nxt:
                pass
        p1_, p2_ = moba_final(h, cbi, qi0, nt, po)
        late.append([SKEW + 1, p1_])
        late.append([SKEW + 22, p2_])
    run_pend()
    flush_late()
    if upto >= 2:
        for _ in range(80):
            pc_step(1)
    P.barrier()
    A.off = base_mark
    if upto <= 2:
        return finish(nc, P, A)

    W1 = A.alloc("W1", [32, 256], BF16)
    w1st = A.alloc("w1st", [4, 256], F32)
    W2 = [A.alloc("W2%d" % i, [2, 64], BF16) for i in range(2)]
    w2st = A.alloc("w2st", [2, 64], F32)
    posT = A.alloc("posT", [32], BF16)
    posst = A.alloc("posst", [32], F32)
    hbias = [A.alloc("hbias%d" % i, [2], F32) for i in range(2)]
    srcT = A.alloc("srcT", [L], BF16)
    srcD = A.alloc("srcD", [16, 512], BF16)
    hid = [A.alloc("hid%d" % i, [2, 512], BF16) for i in range(2)]
    KC = A.alloc("KC", [512], BF16)
    VCp = A.alloc("VCp", [4, 65], BF16)
    KS = A.alloc("KS", [L], BF16)
    KW = A.alloc("KW", [L], BF16)
    VS = A.alloc("VS", [64 * 65], BF16)
    VW = A.alloc("VW", [64 * 65], BF16)
    QG = A.alloc("QG", [4, NO], BF16)
    TBs = A.alloc("TBs", [2, 4, 128], F32)
    TBw = A.alloc("TBw", [5, 4, 128], F32)
    sa_t = [A.alloc("sa%d" % i, [128], F32) for i in range(2)]
    sv_t = [A.alloc("sv%d" % i, [128], F32) for i in range(2)]
    cmt_t = [A.alloc("cmt%d" % i, [512], BF16) for i in range(2)]
    cmT_t = [A.alloc("cmT%d" % i, [4, 128], BF16) for i in range(2)]
    TBsb = A.alloc("TBsb", [2, 4, 128], BF16)
    TBwb = A.alloc("TBwb", [5, 4, 128], BF16)
    GRt = [A.alloc("GRt%d" % i, [3, 4, 128], F32) for i in range(2)]
    pgx = A.alloc("pgx", [516], F32)
    pe_t = [A.alloc("pe%d" % i, [512], F32) for i in range(3)]
    rs_l = [A.alloc("rs%d" % i, [2], F32) for i in range(4)]
    impb = A.alloc("imp", [128], F32)
    t1b = A.alloc("t1b", [128], F32)
    t2b = A.alloc("t2b", [128], F32)
    wkb = A.alloc("wkb", [128], F32)
    selb = A.alloc("selb", [128], F32)
    m16 = A.alloc("m16", [16], F32)
    nmc = A.alloc("nmc", [192], F32)
    QNsA = [A.alloc("QNsA%d" % i, [4, 128], BF16) for i in range(2)]
    QNsB = [A.alloc("QNsB%d" % i, [4, 128], BF16) for i in range(2)]
    QNw = [A.alloc("QNw%d" % i, [4, 128], BF16) for i in range(2)]
    ptc = [A.alloc("ptc%d" % i, [512], BF16) for i in range(5)]
    obr = [A.alloc("obr%d" % i, [512], F32) for i in range(5)]
    oacc = A.alloc("oacc", [512], F32)
    otmp = A.alloc("otmp", [512], F32)
    obb = [A.alloc("obb%d" % i, [512], BF16) for i in range(2)]

    MEMSET("pool", pgx[:], 0.0, [pgx])
    MEMSET("pool", nmc[:], 0.0, [nmc])
    MEMSET("pool", VCp[:, :, 64:65], 1.0, [VCp])
    for i in range(2):
        MEMSET("pool", hid[i][:], 0.0, [hid[i]])
        MEMSET("pool", QNw[i][64:65, :, :], NEG, [QNw[i]])
    for i, (w1n, w2n, pn) in enumerate((("cmp_w1_k", "cmp_w2_k", "cmp_pos_k"), ("cmp_w1_v", "cmp_w2_v", "cmp_pos_v"))):
        w1v = IN[w1n].rearrange("(j d) n -> d j n", d=64)
        p0 = 64 * i
        for jh in range(8):
            DMA("sp", w1st[p0:p0 + 64, :, :], w1v[:, jh * 4:(jh + 1) * 4, :], [], [w1st])
            CP("dve", W1[p0:p0 + 64, jh * 4:(jh + 1) * 4, :], w1st[p0:p0 + 64, :, :], [w1st], [W1])
        DMA("sp", w2st[:], IN[w2n].rearrange("(hf p) e -> p hf e", p=128), [], [w2st])
        CP("dve", W2[i][:], w2st[:], [w2st], [W2[i]])
        DMA("sp", posst[p0:p0 + 64, :], IN[pn].rearrange("j d -> d j"), [], [posst], slow=True)
        CP("dve", posT[p0:p0 + 64, :], posst[p0:p0 + 64, :], [posst], [posT])
        for half in range(2):
            pb_ = psF[4]
            for j in range(32):
                MM(pb_[:, 0:1], W1[p0:p0 + 64, j, half * 128:(half + 1) * 128], posT[p0:p0 + 64, j:j + 1], j == 0, j == 31, [W1, posT], [pb_])
            CP("dve", hbias[i][:, half:half + 1], pb_[:, 0:1], [pb_], [hbias[i]])

    ucnt = [0]
    bcnt = [0]
    pend = []

    late = []

    def run_pend(keep=0):
        while len(pend) > keep:
            pend.pop(0)()
        for it in late:
            it[0] -= 1
        i = 0
        while i < len(late):
            if late[i][0] <= 0:
                late.pop(i)[1]()
            else:
                i += 1

    def flush_late():
        while late:
            late.pop(0)[1]()

    psSc = [psF[0], psF[1], psS[2]]
    psAcc = [psF[2], psF[3], psS[3]]
    SKC = 2

    def unit(lhsT, rhs, r_l, r_r, bias_ap, bias_buf, vlhsT, vbuf, po, first, last):
        u = ucnt[0]
        ucnt[0] += 1
        ps = psSc[u % 3]
        pt_ = ptc[u % 5]
        if bias_ap is not None:
            MM(ps[:, 0:512], lhsT, rhs, True, False, [r_l, r_r], [ps])
            MM(ps[:, 0:512], ident_b[:], bias_ap, False, True, [ident_b, bias_buf], [ps])
        else:
            MM(ps[:, 0:512], lhsT, rhs, True, True, [r_l, r_r], [ps])
        ACT(pt_[:], ps[:, 0:512], AF.Exp, [ps], [pt_])
        run_pend(SKC - 1)
        pend.append(lambda: MM(po[0:65, 0:512], vlhsT, pt_[:], first, last, [vbuf, pt_], [po]))

    def finalize(po, br, grt, tail):
        ob = obr[bcnt[0] % 5]
        bcnt[0] += 1

        def part1():
            CP("dve", ob[0:65, :], po[0:65, 0:512], [po], [ob])
            TS(ob[64:65, :], ob[64:65, :], 1e-30, None, ALU.max, None, [ob], [ob])
            P.op("dve", lambda e, ob=ob: e.reciprocal(out=ob[64:65, :], in_=ob[64:65, :]), r=[ob], w=[ob])
            TT(ob[64:65, :], ob[64:65, :], grt[64:65, br, :, :].rearrange("p h q -> p (h q)"), ALU.mult, [ob, grt], [ob])

        def part2():
            pbc = psF[4]
            MM(pbc[0:64, 0:512], ones_f[64:65, 0:64], ob[64:65, :], True, True, [ones_f, ob], [pbc])
            if br == 0:
                TT(oacc[0:64, :], ob[0:64, :], pbc[0:64, 0:512], ALU.mult, [ob, pbc], [oacc])
            else:
                TT(otmp[0:64, :], ob[0:64, :], pbc[0:64, 0:512], ALU.mult, [ob, pbc], [otmp])
                if tail is None:
                    TT(oacc[0:64, :], oacc[0:64, :], otmp[0:64, :], ALU.add, [oacc, otmp], [oacc])
                else:
                    tail()
        late.append([SKC + 1, part1])
        late.append([SKC + 22, part2])

    OBv = OB.rearrange("h d t -> d h t")

    def front(g, qi):
        c0 = 8 + 4 * g
        o0 = 128 * qi
        qb = qi % 2
        sa, sv, cmt, cmT, grt = sa_t[qb], sv_t[qb], cmt_t[qb], cmT_t[qb], GRt[qb]
        DMA("sp", sa[:], IN["sel_add"][:, qi, :], [], [sa])
        DMA("sp", sv[:], IN["sel_valid"][:, qi, :], [], [sv])
        DMA("sp", cmt[:], IN["cm_tm"][qi, :, :], [], [cmt])
        DMA("sp", cmT[:], IN["cm_T"][qi, :, :, :], [], [cmT])
        DMA("sp", grt[64:65, :, :, :], NG.rearrange("(b h) t -> b h t", b=3)[:, 4 * g:4 * g + 4, o0:o0 + 128].rearrange("(o b) h t -> o b h t", o=1), [], [grt])
        qa, qbB, qw = QNsA[qb], QNsB[qb], QNw[qb]
        qsrc = QG[0:64, :, o0:o0 + 128]
        CP("pool", qa[0:64, :, :], qsrc, [QG], [qa])
        CP("pool", qbB[0:64, :, :], qsrc, [QG], [qbB])
        CP("pool", qw[0:64, :, :], qsrc, [QG], [qw])
        yield
        for hh in range(4):
            pc = psF[5]
            pe = pe_t[(4 * qi + hh) % 3]
            rs_t = rs_l[hh]
            MM(pc[:, 0:512], QG[0:64, hh, o0:o0 + 128], KC[0:64, :], True, False, [QG, KC], [pc])
            MM(pc[:, 0:512], ident_b[:], cmt[:], False, True, [ident_b, cmt], [pc])
            ACT(pe[:], pc[:, 0:512], AF.Exp, [pc], [pe, rs_t], accum_out=rs_t[:, 0:1])
            TS(rs_t[:, 0:1], rs_t[:, 0:1], 1e-30, None, ALU.max, None, [rs_t], [rs_t])
            P.op("dve", lambda e, rs_t=rs_t: e.reciprocal(out=rs_t[:, 1:2], in_=rs_t[:, 0:1]), r=[rs_t], w=[rs_t])
            if hh == 0:
                TS(pgx[:, 1:513], pe[:], rs_t[:, 1:2], None, ALU.mult, None, [pe, rs_t], [pgx])
            else:
                STT(pgx[:, 1:513], pe[:], rs_t[:, 1:2], pgx[:, 1:513], ALU.mult, ALU.add, [pe, rs_t, pgx], [pgx])
            yield
        V = [pgx[:, r:r + 509:4] for r in range(5)]
        TT(t1b[:], V[0], V[4], ALU.add, [pgx], [t1b])
        TT(t2b[:], V[1], V[2], ALU.add, [pgx], [t2b])
        TT(t2b[:], t2b[:], V[3], ALU.add, [pgx, t2b], [t2b])
        STT(impb[:], t2b[:], 2.0, t1b[:], ALU.mult, ALU.add, [t2b, t1b], [impb])
        TT(impb[:], impb[:], sa[:], ALU.add, [impb, sa], [impb])
        P.op("dve", lambda e: e.max(out=m16[:, 0:8], in_=impb[:]), r=[impb], w=[m16])
        P.op("dve", lambda e: e.match_replace(out=wkb[:], in_to_replace=m16[:, 0:8], in_values=impb[:], imm_value=-3e4), r=[m16, impb], w=[wkb])
        P.op("dve", lambda e: e.max(out=m16[:, 8:16], in_=wkb[:]), r=[wkb], w=[m16])
        STT(selb[:], impb[:], m16[:, 15:16], sv[:], ALU.is_ge, ALU.mult, [impb, m16, sv], [selb])
        TS(nmc[:, 64:128], selb[:, 0:64], 30000.0, -30000.0, ALU.mult, ALU.add, [selb], [nmc])
        TS(nmc[:, 128:192], selb[:, 64:128], 30000.0, -30000.0, ALU.mult, ALU.add, [selb], [nmc])
        yield
        yield
        yield
        pt5 = psF[5]
        TR(pt5[:, 0:128], nmc[:, 0:128], ident_f[:], [nmc, ident_f], [pt5])
        TR(pt5[:, 128:256], nmc[:, 64:192], ident_f[:], [nmc, ident_f], [pt5])
        c31b = bcast(c31t[64:128, c0:c0 + 4], 2, 128)
        TT(qa[64:128, :, :], bcast(pt5[64:128, 0:128], 1, 4), c31b, ALU.add, [pt5, c31t], [qa])
        TT(qbB[64:128, :, :], bcast(pt5[64:128, 128:256], 1, 4), c31b, ALU.add, [pt5, c31t], [qbB])
        yield

    for g in (range(2) if upto >= 3 else []):
        run_pend()
        flush_late()
        DMA("sp", srcT[0:64, :], KT[4, 64 * g:64 * g + 64, :], [], [srcT])
        for hh in range(4):
            DMA("pool", QG[0:64, hh, :], NQ[2 * g + hh // 2, 64 * (hh % 2):64 * (hh % 2) + 64, :], [], [QG])
        DMA("act", KS[0:64, :], KT[5, 64 * g:64 * g + 64, :], [], [KS])
        DMA("act", KS[64:128, :], IN["e_sel"][:, :], [], [KS])
        DMA("pool", KW[0:64, :], KT[6, 64 * g:64 * g + 64, :], [], [KW])
        DMA("pool", KW[64:65, :], IN["kinv"][:, :], [], [KW])
        DMA("act", VS[:], VT[8 + g, :, :], [], [VS])
        DMA("pool", VW[:], VT[10 + g, :, :], [], [VW])
        DMA("sp", TBs[:], IN["tb_sel"][g, :, :, :, :], [], [TBs])
        DMA("sp", TBw[:], IN["tb_win"][g, :, :, :, :], [], [TBw])
        c0 = 8 + 4 * g
        for oi in range(2):
            TT(TBsb[:, oi, :, :], TBs[:, oi, :, :], bcast(c31t[:, c0:c0 + 4], 2, 128), ALU.subtract, [TBs, c31t], [TBsb])
        CP("dve", TBwb[:], TBw[:], [TBw], [TBwb])
        for i, kgsrc in enumerate((4, 7)):
            p0 = 64 * i
            if i == 1:
                DMA("sp", srcT[p0:p0 + 64, :], KT[kgsrc, 64 * g:64 * g + 64, :], [], [srcT])
            CP("dve", srcD[p0:p0 + 64, :, :], srcT[p0:p0 + 64, :].rearrange("p (n r) -> p r n", r=16), [srcT], [srcD])
            for half in range(2):
                ph = psF[half]
                for j in range(32):
                    MM(ph[:, 0:511], W1[p0:p0 + 64, j, half * 128:(half + 1) * 128], srcD[p0:p0 + 64, j % 16, (j // 16):(j // 16) + 511], j == 0, j == 31, [W1, srcD], [ph])
                ACT(hid[i][:, half, 0:511], ph[:, 0:511], AF.Gelu_apprx_tanh, [ph, hbias[i]], [hid[i]], bias=hbias[i][:, half:half + 1])
            if i == 0:
                pk = psF[2]
                for half in range(2):
                    MM(pk[0:64, 0:512], W2[0][:, half, :], hid[0][:, half, :], half == 0, half == 1, [W2[0], hid[0]], [pk])
                CP("act", KC[0:64, :], pk[0:64, 0:512], [pk], [KC])
            else:
                for ct in range(4):
                    pv_ = psF[2 + ct % 2]
                    for half in range(2):
                        MM(pv_[:, 0:64], hid[1][:, half, ct * 128:(ct + 1) * 128], W2[1][:, half, :], half == 0, half == 1, [W2[1], hid[1]], [pv_])
                    CP("dve", VCp[:, ct, 0:64], pv_[:, 0:64], [pv_], [VCp])
        for _ in front(g, 0):
            pass
        for qi in range(17):
            lt = 47 + qi
            o0 = 128 * qi
            qb = qi % 2
            cmT, grt = cmT_t[qb], GRt[qb]
            qa, qbB, qw = QNsA[qb], QNsB[qb], QNw[qb]
            nxt = front(g, qi + 1) if qi + 1 < 17 else None
            po = psAcc[(3 * qi) % 3]
            for ct in range(4):
                unit(KC[0:64, ct * 128:(ct + 1) * 128], QG[0:64, :, o0:o0 + 128], KC, QG,
                     bcast(cmT[:, ct, :], 1, 4), cmT, VCp[:, ct, :], VCp, po, ct == 0, ct == 3)
            finalize(po, 0, grt, None)
            po = psAcc[(3 * qi + 1) % 3]
            for kt in range(lt + 1):
                rhs_b = qa if kt < 32 else qbB
                near = kt >= lt - 1
                unit(KS[:, kt * 128:(kt + 1) * 128], rhs_b[:, :, :], KS, rhs_b,
                     TBsb[:, kt - (lt - 1), :, :] if near else None, TBsb, VS[:, kt * 65:(kt + 1) * 65], VS, po, kt == 0, kt == lt)
                if nxt is not None and kt >= 2 and kt % 3 == 0:
                    next(nxt, None)
            finalize(po, 1, grt, None)
            if nxt is not None:
                for _ in nxt:
                    pass
            po = psAcc[(3 * qi + 2) % 3]
            for oi in range(5):
                kt = lt - 4 + oi
                unit(KW[0:65, kt * 128:(kt + 1) * 128], qw[0:65, :, :], KW, qw,
                     TBwb[:, oi, :, :], TBwb, VW[:, kt * 65:(kt + 1) * 65], VW, po, oi == 0, oi == 4)

            def tail(qi=qi, g=g, o0=o0):
                ob_ = obb[qi % 2]
                TT(ob_[0:64, :], oacc[0:64, :], otmp[0:64, :], ALU.add, [oacc, otmp], [ob_])
                DMA("sp", OBv[:, 4 * g:4 * g + 4, o0:o0 + 128], ob_[0:64, :].rearrange("p (h q) -> p h q", h=4), [ob_], [])
            finalize(po, 2, grt, tail)
    run_pend()
    flush_late()
    P.barrier()
    A.off = base_mark
    if upto <= 3:
        return finish(nc, P, A)

    X1 = scr("X1", [NO, D], F32)
    TBLK = [(0, 128), (128, 512), (640, 512), (1152, 512), (1664, 512)]
    WA = A.alloc("WA", [8, D], BF16)
    WB = A.alloc("WB", [8, D], BF16)
    WO = A.alloc("WO", [8, D], BF16)
    wstg = [A.alloc("wstg%d" % i, [D], F32) for i in range(2)]
    OAt = A.alloc("OAt", [8, 512], BF16)
    OBt = A.alloc("OBt", [8, 512], BF16)
    GA = A.alloc("GA", [8, 512], BF16)
    GB = A.alloc("GB", [8, 512], BF16)
    mT = A.alloc("mT", [8, 512], BF16)
    tm1 = [A.alloc("tm1%d" % i, [512], F32) for i in range(2)]
    tm2 = [A.alloc("tm2%d" % i, [512], F32) for i in range(2)]
    xin = [A.alloc("xin%d" % i, [D], F32) for i in range(2)]
    x1t = [A.alloc("x1t%d" % i, [D], F32) for i in range(2)]
    wl = [0]

    def load_w(dst_ap, dst_buf, src_ap, nparts, ncols):
        ws = wstg[wl[0] % 2]
        wl[0] += 1
        DMA("sp", ws[0:nparts, 0:ncols], src_ap, [], [ws])
        CP(("dve", "pool")[wl[0] % 2], dst_ap, ws[0:nparts, 0:ncols], [ws], [dst_buf])

    DMA("sp", WA[:, 0:4, :], WAs.rearrange("(hp r) n -> r hp n", r=128), [], [WA])
    DMA("sp", WB[:, 0:4, :], WBs.rearrange("(hp r) n -> r hp n", r=128), [], [WB])
    DMA("sp", WO[:, :, :], WOs.rearrange("(f p) n -> p f n", p=128), [], [WO])
    OAv = OA.rearrange("h d t -> d h t")
    OBv2 = OB.rearrange("h d t -> d h t")
    GAv = GAB.rearrange("f p t -> p f t")
    xcnt = [0]
    for (o0, N) in TBLK:
        for two in range(2):
            DMA("sp", OAt[64 * two:64 * two + 64, 0:4, 0:N], OA.rearrange("(hp two) d t -> two d hp t", two=2)[two, :, :, o0:o0 + N], [], [OAt])
            DMA("sp", OBt[64 * two:64 * two + 64, 0:4, 0:N], OB.rearrange("(hp two) d t -> two d hp t", two=2)[two, :, :, o0:o0 + N], [], [OBt])
        DMA("sp", GA[:, :, 0:N], GAv[:, 0:8, o0:o0 + N], [], [GA])
        DMA("sp", GB[:, :, 0:N], GAv[:, 8:16, o0:o0 + N], [], [GB])
        for f in range(8):
            pA, pB = psF[f % 2], psF[2 + f % 2]
            for hp in range(4):
                MM(pA[:, 0:N], WA[:, hp, f * 128:(f + 1) * 128], OAt[:, hp, 0:N], hp == 0, hp == 3, [WA, OAt], [pA])
            for hp in range(4):
                MM(pB[:, 0:N], WB[:, hp, f * 128:(f + 1) * 128], OBt[:, hp, 0:N], hp == 0, hp == 3, [WB, OBt], [pB])
            t1_, t2_ = tm1[f % 2], tm2[f % 2]
            TT(t1_[:, 0:N], pA[:, 0:N], GA[:, f, 0:N], ALU.mult, [pA, GA], [t1_])
            TT(t2_[:, 0:N], pB[:, 0:N], GB[:, f, 0:N], ALU.mult, [pB, GB], [t2_])
            TT(mT[:, f, 0:N], t1_[:, 0:N], t2_[:, 0:N], ALU.add, [t1_, t2_], [mT])
        for s_ in range(N // 128):
            xi = xin[xcnt[0] % 2]
            xo = x1t[xcnt[0] % 2]
            xcnt[0] += 1
            r0 = O0 + o0 + s_ * 128
            DMA("sp", xi[:], xl[r0:r0 + 128, :], [], [xi])
            for half in range(2):
                ps = psF[4 + half]
                for f in range(8):
                    MM(ps[:, :], mT[:, f, s_ * 128:(s_ + 1) * 128], WO[:, f, half * 512:(half + 1) * 512], f == 0, f == 7, [mT, WO], [ps])
                TT(xo[:, half * 512:(half + 1) * 512], ps[:, :], xi[:, half * 512:(half + 1) * 512], ALU.add, [ps, xi], [xo])
            DMA("sp", X1[o0 + s_ * 128:o0 + (s_ + 1) * 128, :], xo[:], [xo], [])
    P.barrier()
    A.off = base_mark
    if upto <= 4:
        return finish(nc, P, A)

    WXQ = A.alloc("WXQ", [8, 512], BF16)
    WXO = A.alloc("WXO", [4, D], BF16)
    WXKV = A.alloc("WXKV", [8, D], BF16)
    XK = A.alloc("XK", [4, 256], BF16)
    XV = A.alloc("XV", [2, 4, 128], BF16)
    gXT = A.alloc("gXT", [8], F32)
    gFT = A.alloc("gFT", [8], F32)
    gMT = A.alloc("gMT", [8], F32)
    gO = A.alloc("gO", [D], F32)
    cw = A.alloc("cw", [NFC, 3], F32)
    cb = A.alloc("cb", [NFC], F32)
    carry = A.alloc("carry", [NFC, 2], F32)
    wstg = [A.alloc("wstgb%d" % i, [D], F32) for i in range(2)]
    x1sL = [A.alloc("x1s%d" % i, [4, D], F32) for i in range(2)]
    x1s = x1sL[0]
    junk2 = A.alloc("junk2", [D], BF16)
    xn2 = [A.alloc("xn2%d" % i, [D], BF16) for i in range(4)]
    stat2 = A.alloc("stat2", [12], F32)
    hT2L = [A.alloc("hT2%d" % i, [8, 512], BF16) for i in range(2)]
    hT2 = hT2L[0]
    hTf = A.alloc("hTf", [8, 512], BF16)
    xqT = A.alloc("xqT", [4, 512], BF16)
    ptd = [A.alloc("ptd%d" % i, [512], BF16) for i in range(3)]
    oxs = [A.alloc("oxs%d" % i, [512], F32) for i in range(2)]
    rsx = [A.alloc("rsx%d" % i, [512], F32) for i in range(2)]
    oxT = A.alloc("oxT", [4, 512], BF16)
    WGc = [A.alloc("WGc%d" % i, [8, 128], BF16) for i in range(3)]
    WUc = [A.alloc("WUc%d" % i, [8, 128], BF16) for i in range(3)]
    WDc = [A.alloc("WDc%d" % i, [512], BF16) for i in range(8)]
    aEt = [A.alloc("aE%d" % i, [516], F32) for i in range(2)]
    cvt = [A.alloc("cv%d" % i, [512], F32) for i in range(2)]
    glt = [A.alloc("gl%d" % i, [512], F32) for i in range(2)]
    gTl = [A.alloc("gT%d" % i, [512], BF16) for i in range(NFC)]
    outt = [A.alloc("outt%d" % i, [D], F32) for i in range(2)]

    DMA("sp", WXQ[:, :, :], WXQs.rearrange("(f p) n -> p f n", p=128), [], [WXQ])
    DMA("sp", WXKV[:, :, :], WXKVs.rearrange("(f p) n -> p f n", p=128), [], [WXKV])
    DMA("sp", WXO[:, :, :], WXOs.rearrange("(h p) n -> p h n", p=128), [], [WXO])
    for gt_, nm_ in ((gXT, "norm_xattn_g"), (gFT, "norm_ffn_g"), (gMT, "norm_mem_g")):
        DMA("sp", gt_[:], IN[nm_].rearrange("o (k p) -> p (o k)", p=128), [], [gt_], slow=True)
    DMA("sp", gO[:], IN["norm_final_g"][0:1, :].broadcast_to([128, D]), [], [gO])
    for tap in range(3):
        DMA("sp", cw[:, :, tap:tap + 1], IN["conv_w"][tap:tap + 1, :].rearrange("o (c p) -> p c o", p=128), [], [cw], slow=True)
    DMA("sp", cb[:], IN["conv_b"].rearrange("o (c p) -> p (o c)", p=128), [], [cb], slow=True)
    MEMSET("pool", carry[:], 0.0, [carry])
    for i in range(2):
        MEMSET("pool", aEt[i][:], 0.0, [aEt[i]])
    for s_ in range(2):
        DMA("sp", x1s[:, s_, :], IN["mem"][s_ * 128:(s_ + 1) * 128, :], [], [x1s])
    norm_block([(x1s, x1s[:, s_, :]) for s_ in range(2)], None, junk2, stat2, xn2, hT2, 256, gainT=gMT)
    for hd in range(4):
        ps = psF[hd % 2]
        for f in range(8):
            MM(ps[:, 0:256], WXKV[:, f, hd * 128:(hd + 1) * 128], hT2[:, f, 0:256], f == 0, f == 7, [WXKV, hT2], [ps])
        CP("act", XK[:, hd, :], ps[:, 0:256], [ps], [XK])
    for kt in range(2):
        ps = psF[2 + kt]
        for f in range(8):
            MM(ps[:, :], hT2[:, f, kt * 128:(kt + 1) * 128], WXKV[:, f, 512:1024], f == 0, f == 7, [WXKV, hT2], [ps])
        CP("dve", XV[:, kt, :, :].rearrange("p h d -> p (h d)"), ps[:, :], [ps], [XV])

    XSC = 128.0 ** -0.5
    pendx = []
    ucnt = [0]
    ocnt = [0]

    def d2_load(tb):
        o0, N = TBLK[tb]
        xs_ = x1sL[tb % 2]
        for s_ in range(N // 128):
            DMA("sp", xs_[:, s_, :], X1[o0 + s_ * 128:o0 + (s_ + 1) * 128, :], [], [xs_])

    def d2_norm1(tb):
        o0, N = TBLK[tb]
        xs_ = x1sL[tb % 2]
        norm_block([(xs_, xs_[:, s_, :]) for s_ in range(N // 128)], None, junk2, stat2, xn2, hT2L[tb % 2], N, gainT=gXT)

    d2_load(0)
    d2_norm1(0)
    for tb, (o0, N) in enumerate(TBLK):
        ns = N // 128
        if tb + 1 < len(TBLK):
            d2_load(tb + 1)
        x1s = x1sL[tb % 2]
        hT2 = hT2L[tb % 2]
        for hd in range(4):
            ps = psF[hd % 2]
            for f in range(8):
                MM(ps[:, 0:N], WXQ[:, f, hd * 128:(hd + 1) * 128], hT2[:, f, 0:N], f == 0, f == 7, [WXQ, hT2], [ps])
            CP("act", xqT[:, hd, 0:N], ps[:, 0:N], [ps], [xqT])
        finx = []
        for hd in range(4):
            po = psF[2 + hd % 2]
            psm = psF[4]
            for kt in range(2):
                u = ucnt[0]
                ucnt[0] += 1
                ps = psF[u % 2]
                pt_ = ptd[u % 3]
                MM(ps[:, 0:N], XK[:, hd, kt * 128:(kt + 1) * 128], xqT[:, hd, 0:N], True, True, [XK, xqT], [ps])
                ACT(pt_[:, 0:N], ps[:, 0:N], AF.Exp, [ps], [pt_], scale=XSC)
                while len(pendx) > 0:
                    pendx.pop(0)()

                def pvx(po=po, psm=psm, kt=kt, hd=hd, pt_=pt_, N=N):
                    MM(po[:, 0:N], XV[:, kt, hd, :], pt_[:, 0:N], kt == 0, kt == 1, [XV, pt_], [po])
                    MM(psm[0:1, 0:N], ones_b[:, 0:1], pt_[:, 0:N], kt == 0, kt == 1, [ones_b, pt_], [psm])
                pendx.append(pvx)
                if kt == 0:
                    while finx:
                        finx.pop(0)()
            while len(pendx) > 0:
                pendx.pop(0)()
            ox, rx = oxs[hd % 2], rsx[hd % 2]
            CP("act", ox[:, 0:N], po[:, 0:N], [po], [ox])
            ACT(rx[0:1, 0:N], psm[0:1, 0:N], AF.Ln, [psm], [rx])
            ACT(rx[0:1, 0:N], rx[0:1, 0:N], AF.Exp, [rx], [rx], scale=-1.0)

            def finh(hd=hd, ox=ox, rx=rx, N=N):
                pbc = psF[5]
                MM(pbc[:, 0:N], ones_f[0:1, 0:128], rx[0:1, 0:N], True, True, [ones_f, rx], [pbc])
                TT(oxT[:, hd, 0:N], ox[:, 0:N], pbc[:, 0:N], ALU.mult, [ox, pbc], [oxT])
            finx.append(finh)
        while finx:
            finx.pop(0)()
        for s_ in range(ns):
            for half in range(2):
                ps = psF[half]
                for hd in range(4):
                    MM(ps[:, :], oxT[:, hd, s_ * 128:(s_ + 1) * 128], WXO[:, hd, half * 512:(half + 1) * 512], hd == 0, hd == 3, [oxT, WXO], [ps])
                TT(x1s[:, s_, half * 512:(half + 1) * 512], ps[:, :], x1s[:, s_, half * 512:(half + 1) * 512], ALU.add, [ps, x1s], [x1s])
        norm_block([(x1s, x1s[:, s_, :]) for s_ in range(ns)], None, junk2, stat2, xn2, hTf, N, gainT=gFT)
        for fc in range(NFC):
            wg, wu = WGc[fc % 3], WUc[fc % 3]
            DMA("sp", wg[:], WGs[fc, :, :, :], [], [wg])
            DMA("sp", wu[:], WUs[fc, :, :, :], [], [wu])
            pG, pU = psF[fc % 2], psF[2 + fc % 2]
            for f in range(8):
                MM(pG[:, 0:N], wg[:, f, :], hTf[:, f, 0:N], f == 0, f == 7, [wg, hTf], [pG])
            for f in range(8):
                MM(pU[:, 0:N], wu[:, f, :], hTf[:, f, 0:N], f == 0, f == 7, [wu, hTf], [pU])
            aE, cv, gl = aEt[fc % 2], cvt[fc % 2], glt[fc % 2]
            CP("dve", aE[:, 0:2], carry[:, fc, :], [carry], [aE])
            CP("act", aE[:, 2:2 + N], pG[:, 0:N], [pG], [aE])
            if tb == 0:
                TS(carry[:, fc, :], aE[:, N:N + 2], flag[:, 0:1], None, ALU.mult, None, [aE, flag], [carry])
            else:
                CP("dve", carry[:, fc, :], aE[:, N:N + 2], [aE], [carry])
            TS(cv[:, 0:N], aE[:, 0:N], cw[:, fc, 0:1], None, ALU.mult, None, [aE, cw], [cv])
            STT(cv[:, 0:N], aE[:, 1:N + 1], cw[:, fc, 1:2], cv[:, 0:N], ALU.mult, ALU.add, [aE, cw, cv], [cv])
            STT(cv[:, 0:N], aE[:, 2:N + 2], cw[:, fc, 2:3], cv[:, 0:N], ALU.mult, ALU.add, [aE, cw, cv], [cv])
            ACT(gl[:, 0:N], cv[:, 0:N], AF.Gelu_apprx_tanh, [cv, cb], [gl], bias=cb[:, fc:fc + 1])
            TT(gTl[fc][:, 0:N], gl[:, 0:N], pU[:, 0:N], ALU.mult, [gl, pU], [gTl[fc]])
        for half in range(2):
            for fc in range(NFC):
                wd = WDc[(half * NFC + fc) % 8]
                DMA("sp", wd[:], WDs[fc, :, half * 512:(half + 1) * 512], [], [wd])
                for s_ in range(ns):
                    MM(psF[s_][:, :], gTl[fc][:, s_ * 128:(s_ + 1) * 128], wd[:], fc == 0, fc == NFC - 1, [gTl[fc], wd], [psF[s_]])
            if half == 1 and tb + 1 < len(TBLK):
                d2_norm1(tb + 1)
            for s_ in range(ns):
                TT(x1s[:, s_, half * 512:(half + 1) * 512], psF[s_][:, :], x1s[:, s_, half * 512:(half + 1) * 512], ALU.add, [psF[s_], x1s], [x1s])
        if tb >= 1:
            for s_ in range(ns):
                ACT(junk2[:], x1s[:, s_, :], AF.Square, [x1s], [junk2, stat2], accum_out=stat2[:, s_:s_ + 1])
            ACT(stat2[:, 4:4 + ns], stat2[:, 0:ns], AF.Sqrt, [stat2], [stat2], scale=1.0 / D, bias=EPS_T[:, 0:1])
            P.op("dve", lambda e, ns=ns: e.reciprocal(out=stat2[:, 8:8 + ns], in_=stat2[:, 4:4 + ns]), r=[stat2], w=[stat2])
            for s_ in range(ns):
                ot = outt[ocnt[0] % 2]
                ocnt[0] += 1
                STT(ot[:], x1s[:, s_, :], stat2[:, 8 + s_:9 + s_], gO[:], ALU.mult, ALU.mult, [x1s, stat2, gO], [ot])
                r0 = o0 - 128 + s_ * 128
                DMA("sp", out[r0:r0 + 128, :], ot[:], [ot], [])
    return finish(nc, P, A)


def finish(nc, P, A):
    P.finish_waits("sp")
    with nc.Block() as block:
        P.run(block)
    print("build: ninst=%d nwaits=%d arena_peak=%d" % (P.ninst, P.nwaits, A.peak))
    return nc


def make_in_maps(inputs):
    x = np.asarray(inputs["x"], np.float32)
    rel_bias = np.asarray(inputs["rel_bias"], np.float32)
    maps = []
    tabs = [host_tables(rel_bias, c) for c in range(4)]
    for core in range(8):
        b, c = core // 4, core % 4
        pad = 2048 * (3 - c)
        xl = np.zeros((L, D), np.float32)
        xl[pad:] = x[b, :2048 * (c + 1)]
        m = {"xl": xl}
        for n_, s_ in WEIGHT_SPECS:
            a = np.asarray(inputs[n_], np.float32)
            if n_ == "mem":
                a = a[b]
            m[n_] = np.ascontiguousarray(a.reshape(s_))
        for n_, s_, d_ in TABLE_SPECS:
            m[n_] = np.ascontiguousarray(tabs[c][n_])
        maps.append(m)
    return maps


_NC_CACHE = {}


def kernel(**inputs):
    if "nc" not in _NC_CACHE:
        _NC_CACHE["nc"] = build()
    nc = _NC_CACHE["nc"]
    maps = make_in_maps(inputs)
    res = run_bass_kernel_spmd(nc, maps, core_ids=list(range(8)))
    out = np.zeros((2, 8192, D), np.float32)
    for core in range(8):
        b, c = core // 4, core % 4
        out[b, 2048 * c:2048 * (c + 1)] = res.results[core]["out"]
    return out
```
